# Optimizing a Trainium2 kernel written in Bass

```python
import math
import jax, jax.numpy as jnp
from jax import lax
import numpy as np


D_MODEL = 2048
BATCH = 4
SEQ = 4096
DEPTH = 4
DEC_BATCH = 8
DEC_SEQ = 4096
PAST_LEN = 128

HYENA_WIDTH = 1024
HYENA_ORDER = 2
FILTER_HIDDEN = 64
N_BANDS = 16
FILTER_FEAT = 1 + 2 * N_BANDS
FAST_DECAY_PCT = 0.3
SLOW_DECAY_PCT = 1.5
DECAY_TARGET = 1e-2
ATTN_GROUPS = ((128, 1), (512, 4), (2048, 16))
N_GROUPS = len(ATTN_GROUPS)
HEADS_PER_GROUP = 8
HEAD_DIM = 64
N_ATTN_HEADS = N_GROUPS * HEADS_PER_GROUP
ATTN_WIDTH = N_ATTN_HEADS * HEAD_DIM
ATTN_OUT = HEADS_PER_GROUP * HEAD_DIM
REL_BUCKETS = 32
REL_MAX_DIST = 1024
D_FF = 5632
NORM_EPS = 1e-6
N_MOD = 6
IN_COLS = 3 * HYENA_WIDTH + 3 * ATTN_WIDTH + 2 * D_MODEL
NEG_INF = -1e30

kernel_name = 'hybrid_hyena_dilated_attn_encoder'


def rms_norm(x, g):
    xf = x.astype(jnp.float32)
    y = xf * lax.rsqrt(jnp.mean(xf * xf, axis=-1, keepdims=True) + NORM_EPS)
    return (y * g.astype(jnp.float32)).astype(x.dtype)


def dwconv3(x, w, b):
    xp = jnp.pad(x, ((0, 0), (1, 1), (0, 0)))
    return xp[:, :-2] * w[0] + xp[:, 1:-1] * w[1] + xp[:, 2:] * w[2] + b


def hyena_filters(L, w1, b1, fr1, w2, b2, fr2, w3):
    f32 = jnp.float32
    t = jnp.linspace(0.0, 1.0, L, dtype=f32)[:, None]
    pos = jnp.arange(L, dtype=f32)[:, None]
    bands = jnp.linspace(1e-4, N_BANDS - 1, N_BANDS, dtype=f32)[None, :]
    ang = (2.0 * math.pi / L) * pos * bands
    feat = jnp.concatenate([t, jnp.cos(ang), -jnp.sin(ang)], axis=-1)
    h = jnp.sin(fr1.astype(f32) * (feat @ w1.astype(f32) + b1.astype(f32)))
    h = jnp.sin(fr2.astype(f32) * (h @ w2.astype(f32) + b2.astype(f32)))
    h = (h @ w3.astype(f32)).reshape(L, 2, HYENA_ORDER, HYENA_WIDTH)
    max_decay = math.log(DECAY_TARGET) / FAST_DECAY_PCT
    min_decay = math.log(DECAY_TARGET) / SLOW_DECAY_PCT
    deltas = jnp.abs(jnp.linspace(min_decay, max_decay, HYENA_WIDTH, dtype=f32))
    h = h * jnp.exp(-t * deltas)[:, None, None, :]
    k2 = jnp.concatenate([h[:, 0], jnp.zeros((1, HYENA_ORDER, HYENA_WIDTH), f32), h[:0:-1, 1]], axis=0)
    k2 = k2 / jnp.sum(jnp.abs(k2), axis=0, keepdims=True)
    return jnp.fft.rfft(k2, axis=0)


def hyena_mix(u, conv_w, conv_b, K, hy_bias):
    L = u.shape[1]
    f32 = jnp.float32
    uc = dwconv3(u, conv_w, conv_b).astype(f32)
    v, x1, x2 = jnp.split(uc, 3, axis=-1)
    bias = hy_bias.astype(f32)
    z = v
    for o, xg in enumerate((x1, x2)):
        y = jnp.fft.irfft(jnp.fft.rfft(z, n=2 * L, axis=1) * K[None, :, o, :], n=2 * L, axis=1)[:, :L]
        z = xg * (y + bias[o] * z)
    return z.astype(u.dtype)


def t5_bucket(rel):
    nb = REL_BUCKETS // 2
    ret = (rel > 0).astype(np.int32) * nb
    n = np.abs(rel)
    max_exact = nb // 2
    large = max_exact + (np.log(np.maximum(n, 1) / max_exact) / np.log(REL_MAX_DIST / max_exact)
                         * (nb - max_exact)).astype(np.int32)
    large = np.minimum(large, nb - 1)
    return ret + np.where(n < max_exact, n, large)


def dilated_group_attn(q, k, v, rel_bias_g, dil, n_side):
    B, L, H, E = q.shape
    blk = n_side
    chunk = dil * blk
    Lp = -(-L // chunk) * chunk
    M = Lp // dil
    nb = M // blk

    def to_classes(a):
        a = jnp.pad(a, ((0, 0), (0, Lp - L), (0, 0), (0, 0)))
        return a.reshape(B, M, dil, H, E).transpose(0, 2, 3, 1, 4)

    def windows(a):
        a = jnp.pad(a, ((0, 0), (0, 0), (0, 0), (blk, blk), (0, 0))).reshape(B, dil, H, nb + 2, blk, E)
        return jnp.concatenate([a[:, :, :, :-2], a[:, :, :, 1:-1], a[:, :, :, 2:]], axis=4)

    qb = to_classes(q).reshape(B, dil, H, nb, blk, E)
    kw = windows(to_classes(k))
    vw = windows(to_classes(v)).astype(jnp.float32)

    qq = np.arange(blk)[:, None]
    kk = np.arange(3 * blk)[None, :]
    j = kk - blk - qq
    band = np.abs(j) <= n_side
    pos = np.arange(Lp).reshape(M, dil).T
    valid = np.pad(pos < L, ((0, 0), (blk, blk))).reshape(dil, nb + 2, blk)
    kvalid = np.concatenate([valid[:, :-2], valid[:, 1:-1], valid[:, 2:]], axis=2)
    mask = band[None, None] & kvalid[:, :, None, :]
    bias = jnp.transpose(rel_bias_g[t5_bucket(j * dil)], (2, 0, 1)).astype(jnp.float32)

    s = jnp.einsum('bdhnqe,bdhnke->bdhnqk', qb, kw, preferred_element_type=jnp.float32) / math.sqrt(E)
    s = jnp.where(mask[None, :, None], s + bias[None, None, :, None], NEG_INF)
    m = jnp.max(s, axis=-1, keepdims=True)
    p = jnp.exp(s - m)
    den = jnp.sum(p, axis=-1, keepdims=True)
    o = jnp.einsum('bdhnqk,bdhnke->bdhnqe', p, vw) / den
    lse = (m + jnp.log(den))[..., 0]
    o = o.reshape(B, dil, H, M, E).transpose(0, 3, 1, 2, 4).reshape(B, Lp, H, E)[:, :L]
    lse = lse.reshape(B, dil, H, M).transpose(0, 3, 1, 2).reshape(B, Lp, H)[:, :L]
    return o, lse


def dilated_attention(qkv, rel_bias):
    B, L, _ = qkv.shape
    q, k, v = [a.reshape(B, L, N_GROUPS, HEADS_PER_GROUP, HEAD_DIM) for a in jnp.split(qkv, 3, axis=-1)]
    outs, lses = [], []
    for g, (window, dil) in enumerate(ATTN_GROUPS):
        n_side = (window // 2) // dil
        o, lse = dilated_group_attn(q[:, :, g], k[:, :, g], v[:, :, g],
                                    rel_bias[:, g * HEADS_PER_GROUP:(g + 1) * HEADS_PER_GROUP], dil, n_side)
        outs.append(o)
        lses.append(lse)
    wts = jax.nn.softmax(jnp.stack(lses, axis=2), axis=2)
    o = jnp.sum(wts[..., None] * jnp.stack(outs, axis=2), axis=2)
    return o.reshape(B, L, ATTN_OUT).astype(qkv.dtype)


def encoder_trunk(x, c, p):
    L = x.shape[1]
    cs = jax.nn.silu(c)
    for l in range(DEPTH):
        mod = cs @ p['ada_w'][l] + p['ada_b'][l]
        sh1, sc1, g1, sh2, sc2, g2 = jnp.split(mod[:, None, :], N_MOD, axis=-1)
        h = rms_norm(x, p['norm1_g'][l]) * (1 + sc1) + sh1
        u = h @ p['w_in'][l]
        u_hy, u_at, u_gate = jnp.split(u, [3 * HYENA_WIDTH, 3 * HYENA_WIDTH + 3 * ATTN_WIDTH], axis=-1)
        K = hyena_filters(L, p['filt_w1'][l], p['filt_b1'][l], p['filt_freq1'][l],
                          p['filt_w2'][l], p['filt_b2'][l], p['filt_freq2'][l], p['filt_w3'][l])
        y_hy = hyena_mix(u_hy, p['hy_conv_w'][l], p['hy_conv_b'][l], K, p['hy_bias'][l])
        y_at = dilated_attention(u_at, p['rel_bias'])
        g_hy, g_at = jnp.split(jax.nn.sigmoid(u_gate + p['b_gate'][l]), 2, axis=-1)
        merged = g_hy * (y_hy @ p['w_br_hy'][l]) + g_at * (y_at @ p['w_br_attn'][l])
        x = x + g1 * (merged @ p['w_out'][l])
        h = rms_norm(x, p['norm2_g'][l]) * (1 + sc2) + sh2
        a, gt = jnp.split(h @ p['ffn_up'][l], 2, axis=-1)
        gt = dwconv3(gt, p['ffn_conv_w'][l], p['ffn_conv_b'][l])
        x = x + g2 * ((jax.nn.silu(gt) * a) @ p['ffn_down'][l])
    return rms_norm(x, p['final_g'])


def setup_inputs(seed: int = 0) -> dict:
    key = jax.random.key(seed)
    ks = jax.random.split(key, 32)
    f32 = jnp.float32

    def nrm(k, shape, scale):
        return scale * jax.random.normal(k, shape, f32)

    D, HW = D_MODEL, HYENA_WIDTH
    return {
        'x_prompt': nrm(ks[0], (BATCH, SEQ, D), 1.0),
        'x_sample': nrm(ks[1], (DEC_BATCH, DEC_SEQ, D), 1.0),
        'c_prompt': nrm(ks[2], (BATCH, D), 1.0),
        'c_sample': nrm(ks[3], (DEC_BATCH, D), 1.0),
        'ada_w': nrm(ks[4], (DEPTH, D, N_MOD * D), 0.5 * D ** -0.5),
        'ada_b': nrm(ks[5], (DEPTH, N_MOD * D), 0.02),
        'norm1_g': 1.0 + nrm(ks[6], (DEPTH, D), 0.01),
        'w_in': nrm(ks[7], (DEPTH, D, IN_COLS), D ** -0.5),
        'b_gate': nrm(ks[8], (DEPTH, 2 * D), 0.01),
        'hy_conv_w': nrm(ks[9], (DEPTH, 3, 3 * HW), 3 ** -0.5),
        'hy_conv_b': nrm(ks[10], (DEPTH, 3 * HW), 0.01),
        'filt_w1': nrm(ks[11], (DEPTH, FILTER_FEAT, FILTER_HIDDEN), FILTER_FEAT ** -0.5),
        'filt_b1': nrm(ks[12], (DEPTH, FILTER_HIDDEN), 0.01),
        'filt_freq1': 1.0 + nrm(ks[13], (DEPTH, FILTER_HIDDEN), 0.01),
        'filt_w2': nrm(ks[14], (DEPTH, FILTER_HIDDEN, FILTER_HIDDEN), FILTER_HIDDEN ** -0.5),
        'filt_b2': nrm(ks[15], (DEPTH, FILTER_HIDDEN), 0.01),
        'filt_freq2': 1.0 + nrm(ks[16], (DEPTH, FILTER_HIDDEN), 0.01),
        'filt_w3': nrm(ks[17], (DEPTH, FILTER_HIDDEN, 2 * HYENA_ORDER * HW), FILTER_HIDDEN ** -0.5),
        'hy_bias': nrm(ks[18], (DEPTH, HYENA_ORDER, HW), 0.5),
        'rel_bias': nrm(ks[19], (REL_BUCKETS, N_ATTN_HEADS), 0.5),
        'w_br_hy': nrm(ks[20], (DEPTH, HW, D), HW ** -0.5),
        'w_br_attn': nrm(ks[21], (DEPTH, ATTN_OUT, D), ATTN_OUT ** -0.5),
        'w_out': nrm(ks[22], (DEPTH, D, D), D ** -0.5),
        'norm2_g': 1.0 + nrm(ks[23], (DEPTH, D), 0.01),
        'ffn_up': nrm(ks[24], (DEPTH, D, 2 * D_FF), D ** -0.5),
        'ffn_conv_w': nrm(ks[25], (DEPTH, 3, D_FF), 3 ** -0.5),
        'ffn_conv_b': nrm(ks[26], (DEPTH, D_FF), 0.01),
        'ffn_down': nrm(ks[27], (DEPTH, D_FF, D), D_FF ** -0.5),
        'final_g': 1.0 + nrm(ks[28], (D,), 0.01),
    }


def reference(x_prompt, x_sample, c_prompt, c_sample, ada_w, ada_b, norm1_g, w_in, b_gate,
              hy_conv_w, hy_conv_b, filt_w1, filt_b1, filt_freq1, filt_w2, filt_b2, filt_freq2,
              filt_w3, hy_bias, rel_bias, w_br_hy, w_br_attn, w_out, norm2_g, ffn_up,
              ffn_conv_w, ffn_conv_b, ffn_down, final_g):
    params = {
        'ada_w': ada_w, 'ada_b': ada_b, 'norm1_g': norm1_g, 'w_in': w_in, 'b_gate': b_gate,
        'hy_conv_w': hy_conv_w, 'hy_conv_b': hy_conv_b, 'filt_w1': filt_w1, 'filt_b1': filt_b1,
        'filt_freq1': filt_freq1, 'filt_w2': filt_w2, 'filt_b2': filt_b2, 'filt_freq2': filt_freq2,
        'filt_w3': filt_w3, 'hy_bias': hy_bias, 'rel_bias': rel_bias, 'w_br_hy': w_br_hy,
        'w_br_attn': w_br_attn, 'w_out': w_out, 'norm2_g': norm2_g, 'ffn_up': ffn_up,
        'ffn_conv_w': ffn_conv_w, 'ffn_conv_b': ffn_conv_b, 'ffn_down': ffn_down, 'final_g': final_g,
    }
    y_prompt = encoder_trunk(x_prompt, c_prompt, params)
    y_sample = encoder_trunk(x_sample, c_sample, params)
    return (y_prompt, y_sample)
```

```python
import math
import contextlib
import numpy as np
import ml_dtypes
import concourse.bass as bass
import concourse.mybir as mybir
from concourse.bass_utils import run_bass_kernel_spmd

F32 = mybir.dt.float32
BF16 = mybir.dt.bfloat16
AF = mybir.ActivationFunctionType
ALU = mybir.AluOpType

D = 2048
L = 4096
DEPTH = 4
HW = 1024
AW = 1536
INC = 3 * HW + 3 * AW + 2 * D
DFF = 5632
NFFT = 2 * L
GROUPS = ((128, 1), (512, 4), (2048, 16))
EPS = 1e-6
WLEN = 383


class Res:
    __slots__ = ("w", "r", "base", "excl")

    def __init__(self):
        self.w = {}
        self.r = {}
        self.base = {}
        self.excl = False


class Tile:
    def __init__(self, stk, nc, shape, dt, psum=False):
        if psum:
            if dt == BF16 and list(shape) == [128, 512]:
                shape = [128, 1024]
            self.t = stk.enter_context(nc.psum_tensor(list(shape), dt))
        else:
            self.t = stk.enter_context(nc.sbuf_tensor(list(shape), dt))
        self.r = Res()
        self.r.excl = psum


class Sched:
    ENG = ("pe", "act", "dve", "pool", "sp")
    NP = 12

    def __init__(self, nc, stk):
        self.nc = nc
        self.ops = {e: [] for e in self.ENG}
        self.csem = {e: stk.enter_context(nc.semaphore("c_" + e)) for e in ("pe", "act", "dve", "pool")}
        self.cnt = {e: 0 for e in self.csem}
        self.dsem = {q: [stk.enter_context(nc.semaphore("d_%s%d" % (q, i))) for i in range(self.NP)]
                     for q in ("sp", "pool")}
        self.duse = {q: [0] * self.NP for q in ("sp", "pool")}
        self.dnext = {q: 0 for q in ("sp", "pool")}
        self.waited = {e: {} for e in self.ENG}
        self.bar = stk.enter_context(nc.semaphore("bar"))
        self.nbar = 0

    def _deps(self, reads, writes, pwrites):
        deps = {}

        def add(d):
            for s, v in d.items():
                if deps.get(s, 0) < v:
                    deps[s] = v
        for x in reads:
            add(x.w)
            if x.excl:
                add(x.r)
        for x in writes:
            add(x.w)
            add(x.r)
        for x in pwrites:
            if x.r:
                x.base = x.r
                x.r = {}
                x.w = {}
            add(x.base)
        return deps

    def _commit(self, tok, reads, writes, pwrites):
        s, v = tok
        for x in reads:
            if x.r.get(s, 0) < v:
                x.r[s] = v
        for x in writes:
            x.w = {s: v}
            x.r = {}
            x.base = {s: v}
        for x in pwrites:
            if x.w.get(s, 0) < v:
                x.w[s] = v

    def _waits(self, eng, deps):
        wd = self.waited[eng]
        out = []
        for s, v in deps.items():
            if wd.get(s, 0) < v:
                wd[s] = v
                out.append((s, v))
        return out

    def op(self, eng, fn, reads=(), writes=(), pwrites=()):
        deps = self._deps(reads, writes, pwrites)
        if eng == "pe":
            deps.pop(self.csem["pe"], None)
        waits = self._waits(eng, deps)
        self.cnt[eng] += 1
        tok = (self.csem[eng], self.cnt[eng])
        self.ops[eng].append((waits, fn, tok[0], 1))
        self._commit(tok, reads, writes, pwrites)

    def dma(self, q, out, in_, reads=(), writes=(), pwrites=()):
        deps = self._deps(reads, writes, pwrites)
        i = self.dnext[q]
        self.dnext[q] = (i + 1) % self.NP
        sem = self.dsem[q][i]
        if self.duse[q][i] > 0:
            v = 16 * self.duse[q][i]
            if deps.get(sem, 0) < v:
                deps[sem] = v
        waits = self._waits(q, deps)
        self.duse[q][i] += 1
        tok = (sem, 16 * self.duse[q][i])
        self.ops[q].append((waits, (lambda e, o=out, i_=in_: e.dma_start(out=o, in_=i_)), sem, 16))
        self._commit(tok, reads, writes, pwrites)

    def barrier(self):
        deps = {}
        for e, s in self.csem.items():
            if self.cnt[e]:
                deps[s] = self.cnt[e]
        for q in self.dsem:
            for i, s in enumerate(self.dsem[q]):
                if self.duse[q][i]:
                    deps[s] = 16 * self.duse[q][i]
        waits = self._waits("sp", dict(deps))
        self.nbar += 1
        bar, n = self.bar, self.nbar
        self.ops["sp"].append((waits, (lambda e: e.sem_inc(bar, 1)), None, 0))
        for e in ("pe", "act", "dve", "pool"):
            self.ops[e].append(([(bar, n)], None, None, 0))
            wd = self.waited[e]
            for s, v in deps.items():
                if wd.get(s, 0) < v:
                    wd[s] = v

    def emit(self, block):
        nc = self.nc

        def run(eng_name):
            def body(e):
                for waits, fn, sem, amt in self.ops[eng_name]:
                    for s, v in waits:
                        e.wait_ge(s, v)
                    if fn is not None:
                        ins = fn(e)
                        if sem is not None:
                            ins.then_inc(sem, amt)
            return body
        block.tensor(run("pe"))
        block.scalar(run("act"))
        block.vector(run("dve"))
        block.gpsimd(run("pool"))
        block.sync(run("sp"))


def f_copy(eng, out, in_):
    if eng == "act":
        return lambda e: e.activation(out=out, in_=in_, func=AF.Copy)
    return lambda e: e.tensor_copy(out=out, in_=in_)


def f_act(out, in_, func, bias=None, scale=None):
    kw = {}
    if bias is not None:
        kw["bias"] = bias
    if scale is not None:
        kw["scale"] = scale
    return lambda e: e.activation(out=out, in_=in_, func=func, **kw)


def f_tt(out, a, b, op):
    return lambda e: e.tensor_tensor(out=out, in0=a, in1=b, op=op)


def f_ts(out, a, s1, s2, op0, op1=None):
    if op1 is None:
        return lambda e: e.tensor_scalar(out=out, in0=a, scalar1=s1, scalar2=None, op0=op0)
    return lambda e: e.tensor_scalar(out=out, in0=a, scalar1=s1, scalar2=s2, op0=op0, op1=op1)


def f_stt(out, in0, scalar, in1, op0, op1):
    return lambda e: e.scalar_tensor_tensor(out=out, in0=in0, scalar=scalar, in1=in1, op0=op0, op1=op1)


def f_mms(items):
    def fn(e):
        ins = None
        for (o, l, r, a, b) in items:
            ins = e.matmul(o, l, r, start=a, stop=b)
        return ins
    return fn


def f_trs(items, ident):
    def fn(e):
        ins = None
        for (o, i) in items:
            ins = e.transpose(o, i, ident)
        return ins
    return fn


def f_memset(ap, v):
    return lambda e: e.memset(ap, v)


class Env:
    pass


ATTN_DBG = {"eb": True, "deint": True, "vtr": True, "blocks": True, "tail": True}


def phase_consts(S, nc, env):
    k = env.pstk
    nseq, depth = env.nseq, env.depth

    def ld(name, shape, dt, src):
        t = Tile(k, nc, shape, dt)
        S.dma("sp", t.t[:], src, writes=[t.r])
        setattr(env, name, t)
    ld("ident_f", [128, 128], F32, env.i_ident_f)
    ld("ident_b", [128, 128], BF16, env.i_ident_b)
    ld("n1g", [128, depth, 16], F32, env.i_n1g)
    ld("n2g", [128, depth, 16], F32, env.i_n2g)
    ld("fing", [128, 16], F32, env.i_fing)
    ld("bgate", [128, depth, 32], F32, env.i_bgate)
    ld("hcw", [128, depth, 24, 3], F32, env.i_hcw)
    ld("hcb", [128, depth, 24], F32, env.i_hcb)
    ld("hbT", [128, depth, 2, 8], F32, env.i_hbT)
    ld("fcw", [128, depth, 44, 3], F32, env.i_fcw)
    ld("fcb", [128, depth, 44], F32, env.i_fcb)
    ld("adab", [128, depth, 96], F32, env.i_adab)
    env.modT = Tile(k, nc, [128, depth, 96, nseq], F32)
    env.rn = Tile(k, nc, [128, depth, 16], F32)
    env.ones_b = Tile(k, nc, [128, 128], BF16)
    env.ones_f = Tile(k, nc, [128, 128], F32)
    env.ov = Tile(k, nc, [128, 6, 128], BF16)
    env.negpi = Tile(k, nc, [128, 1], F32)
    env.epsb = Tile(k, nc, [128, 1], F32)
    S.op("dve", f_memset(env.ones_b.t[:], 1.0), writes=[env.ones_b.r])
    S.op("dve", f_memset(env.ones_f.t[:], 1.0), writes=[env.ones_f.r])
    S.op("dve", f_memset(env.negpi.t[:], -math.pi * (1 - 2e-6)), writes=[env.negpi.r])
    S.op("dve", f_memset(env.epsb.t[:], EPS), writes=[env.epsb.r])
    S.op("pool", f_memset(env.ov.t[:], 0.0), writes=[env.ov.r])
    for v in range(3):
        for h in range(2):
            lo, hi = (0, 128) if v == 0 else ((64, 128) if v == 1 else (0, 64))
            S.op("pool", f_memset(env.ov.t[lo:hi, 2 * v + h, h * 64:(h + 1) * 64], 1.0), writes=[env.ov.r])
    S.barrier()


def phase_xT(S, nc, env, s):
    with contextlib.ExitStack() as k:
        xi = [Tile(k, nc, [128, D], F32) for _ in range(2)]
        xo = [Tile(k, nc, [128, 16, 512], F32) for _ in range(2)]
        ps = [Tile(k, nc, [128, 512], F32, psum=True) for _ in range(4)]
        ident = env.ident_f
        xTv = env.xT[s].rearrange("(dc p) t -> p dc t", p=128)
        n = 0
        for tt in range(L // 512):
            o = xo[tt % 2]
            for q in range(4):
                tc = tt * 4 + q
                x = xi[tc % 2]
                S.dma("sp", x.t[:], env.xin[s, tc * 128:(tc + 1) * 128, :], writes=[x.r])
                for g4 in range(4):
                    p = ps[n % 4]
                    n += 1
                    items = [(p.t[:, j * 128:(j + 1) * 128], x.t[:, (g4 * 4 + j) * 128:(g4 * 4 + j + 1) * 128])
                             for j in range(4)]
                    S.op("pe", f_trs(items, ident.t[:]), reads=[x.r, ident.r], writes=[p.r])
                    eng = "act" if g4 % 2 else "dve"
                    S.op(eng, f_copy(eng, o.t[:, g4 * 4:(g4 + 1) * 4, q * 128:(q + 1) * 128],
                                     p.t[:].rearrange("p (j t) -> p j t", j=4)),
                         reads=[p.r], pwrites=[o.r])
            S.dma("sp", xTv[:, :, tt * 512:(tt + 1) * 512], o.t[:], reads=[o.r])
        S.barrier()


def phase_mod(S, nc, env):
    nseq, depth = env.nseq, env.depth
    with contextlib.ExitStack() as k:
        cs = Tile(k, nc, [128, 16, nseq], F32)
        wt = [Tile(k, nc, [128, 16, 512], F32) for _ in range(3)]
        ps = Tile(k, nc, [128, 512], F32, psum=True)
        S.dma("sp", cs.t[:], env.i_cT, writes=[cs.r])
        S.op("act", f_act(cs.t[:], cs.t[:], AF.Silu), reads=[cs.r], writes=[cs.r])
        n = 0
        for l in range(depth):
            wv = env.i_ada_w[l].rearrange("(kc p) n -> p kc n", p=128)
            for ct in range(24):
                w = wt[n % 3]
                n += 1
                S.dma("sp", w.t[:], wv[:, :, ct * 512:(ct + 1) * 512], writes=[w.r])
                items = []
                for j in range(4):
                    jg = ct * 4 + j
                    for kc in range(16):
                        items.append((ps.t[:, jg * nseq:(jg + 1) * nseq], w.t[:, kc, j * 128:(j + 1) * 128],
                                      cs.t[:, kc, :], kc == 0, kc == 15))
                S.op("pe", f_mms(items), reads=[w.r, cs.r], writes=[ps.r])
            m = env.modT
            psv = ps.t[:, 0:96 * nseq].rearrange("p (j s) -> p j s", s=nseq)
            for s in range(nseq):
                S.op("dve", f_tt(m.t[:, l, :, s], psv[:, :, s], env.adab.t[:, l, :], ALU.add),
                     reads=[ps.r, env.adab.r], writes=[m.r])
                S.op("dve", f_stt(m.t[:, l, 16:32, s], m.t[:, l, 16:32, s], 1.0, env.n1g.t[:, l, :], ALU.add, ALU.mult),
                     reads=[m.r, env.n1g.r], writes=[m.r])
                S.op("dve", f_stt(m.t[:, l, 64:80, s], m.t[:, l, 64:80, s], 1.0, env.n2g.t[:, l, :], ALU.add, ALU.mult),
                     reads=[m.r, env.n2g.r], writes=[m.r])
        S.barrier()


def phase_eb1(S, nc, env):
    with contextlib.ExitStack() as k:
        rb = Tile(k, nc, [32, 24], F32)
        sel = Tile(k, nc, [32, 3, WLEN], F32)
        lt = [Tile(k, nc, [32, 128], F32) for _ in range(2)]
        wsb = [Tile(k, nc, [128, WLEN], F32) for _ in range(2)]
        ps = [Tile(k, nc, [128, 512], F32, psum=True) for _ in range(2)]
        S.dma("sp", rb.t[:], env.i_rel_bias, writes=[rb.r])
        S.dma("sp", sel.t[:], env.i_selw, writes=[sel.r])
        S.op("act", f_act(rb.t[:], rb.t[:], AF.Exp), reads=[rb.r], writes=[rb.r])
        for gh in range(24):
            g = gh // 8
            a = lt[gh % 2]
            p = ps[gh % 2]
            w = wsb[gh % 2]
            S.op("dve", f_ts(a.t[:], env.ones_f.t[0:32, :], rb.t[:, gh:gh + 1], None, ALU.mult),
                 reads=[env.ones_f.r, rb.r], writes=[a.r])
            S.op("pe", f_mms([(p.t[:, 0:WLEN], a.t[:], sel.t[:, g, :], True, True)]), reads=[a.r, sel.r], writes=[p.r])
            S.op("act", f_copy("act", w.t[:], p.t[:, 0:WLEN]), reads=[p.r], writes=[w.r])
            S.dma("sp", env.Wd[gh], w.t[:], reads=[w.r])
        S.barrier()


def load_eb(S, nc, env, k):
    ebf = [Tile(k, nc, [128, 2, 128], F32) for _ in range(2)]
    eb = Tile(k, nc, [128, 24, 2, 128], BF16)
    wh = env.Wd_h
    for gh in range(24):
        st_ = ebf[gh % 2]
        for c in range(2):
            off = gh * 128 * WLEN + (255 if c == 0 else 127)
            src = bass.AP(tensor=wh, offset=off, ap=[[WLEN - 1, 128], [1, 128]])
            S.dma("sp", st_.t[:, c, :], src, pwrites=[st_.r])
        S.op("dve", f_copy("dve", eb.t[:, gh, :, :], st_.t[:]), reads=[st_.r], pwrites=[eb.r])
    return eb


def sin_layer(S, nc, env, k, ps, dst, dres, bcol, fcol, tmp, tmp2, fpr):
    MAGIC = 12582912.0
    S.op("dve", f_ts(tmp.t[0:64, :], ps.t[0:64, :], bcol, fcol, ALU.add, ALU.mult), reads=[ps.r, fpr], writes=[tmp.r])
    S.op("dve", f_ts(tmp2.t[0:64, :], tmp.t[0:64, :], 1.0 / (2 * math.pi), MAGIC, ALU.mult, ALU.add),
         reads=[tmp.r], writes=[tmp2.r])
    S.op("dve", f_ts(tmp2.t[0:64, :], tmp2.t[0:64, :], MAGIC, -2 * math.pi, ALU.subtract, ALU.mult),
         reads=[tmp2.r], writes=[tmp2.r])
    S.op("dve", f_tt(tmp.t[0:64, :], tmp.t[0:64, :], tmp2.t[0:64, :], ALU.add), reads=[tmp.r, tmp2.r], writes=[tmp.r])
    S.op("act", f_act(dst, tmp.t[0:64, :], AF.Sin, scale=(1 - 2e-6)), reads=[tmp.r], pwrites=[dres])


def phase_filter(S, nc, env, l):
    with contextlib.ExitStack() as k:
        featT = Tile(k, nc, [33, L], F32)
        w1 = Tile(k, nc, [33, 64], F32)
        w2 = Tile(k, nc, [64, 64], F32)
        w3 = Tile(k, nc, [64, 4096], F32)
        fp = Tile(k, nc, [64, 4], F32)
        h1 = Tile(k, nc, [64, L], F32)
        h2 = Tile(k, nc, [64, L], F32)
        tmp = [Tile(k, nc, [128, 512], F32) for _ in range(2)]
        dec = [Tile(k, nc, [128, 512], F32) for _ in range(2)]
        hf = [Tile(k, nc, [128, 512], F32) for _ in range(2)]
        hb = [Tile(k, nc, [128, 512], F32) for _ in range(2)]
        acc = Tile(k, nc, [128, 512], F32)
        ee = Tile(k, nc, [128, 32, 512], BF16)
        dd = Tile(k, nc, [128, 32, 512], BF16)
        Ft = [Tile(k, nc, [128, 32, 128], BF16) for _ in range(3)]
        ko = [Tile(k, nc, [128, 512], F32) for _ in range(2)]
        alt = Tile(k, nc, [128, 2], BF16)
        nyq = Tile(k, nc, [1, 512], F32)
        ps = [Tile(k, nc, [128, 512], F32, psum=True) for _ in range(6)]
        psn = Tile(k, nc, [128, 512], F32, psum=True)
        S.dma("sp", featT.t[:], env.i_featT, writes=[featT.r])
        S.dma("sp", w1.t[:], env.i_fw1[l], writes=[w1.r])
        S.dma("sp", w2.t[:], env.i_fw2[l], writes=[w2.r])
        S.dma("sp", w3.t[:], env.i_fw3[l], writes=[w3.r])
        S.dma("sp", fp.t[:], env.i_fpar[l], writes=[fp.r])
        S.dma("sp", alt.t[:], env.i_alt, writes=[alt.r])
        for tt in range(8):
            p = ps[tt % 2]
            S.op("pe", f_mms([(p.t[0:64, :], w1.t[:], featT.t[:, tt * 512:(tt + 1) * 512], True, True)]),
                 reads=[w1.r, featT.r], writes=[p.r])
            sin_layer(S, nc, env, k, p, h1.t[:, tt * 512:(tt + 1) * 512], h1.r, fp.t[:, 0:1], fp.t[:, 1:2], tmp[tt % 2], hf[tt % 2], fp.r)
        for tt in range(8):
            p = ps[tt % 2]
            S.op("pe", f_mms([(p.t[0:64, :], w2.t[:], h1.t[:, tt * 512:(tt + 1) * 512], True, True)]),
                 reads=[w2.r, h1.r], writes=[p.r])
            sin_layer(S, nc, env, k, p, h2.t[:, tt * 512:(tt + 1) * 512], h2.r, fp.t[:, 2:3], fp.t[:, 3:4], tmp[tt % 2], hf[tt % 2], fp.r)
        if env.dbg and l == 0:
            S.dma("sp", env.dbg_h1, h1.t[:], reads=[h1.r])
            S.dma("sp", env.dbg_h2, h2.t[:], reads=[h2.r])
        n = 0
        nf = 0
        nk = 0
        for ct in range(4):
            o, ch = ct // 2, ct % 2
            cf = o * 1024 + ch * 512
            cb = 2048 + cf
            for lc in range(32):
                pf, pb = ps[(2 * n) % 4], ps[(2 * n + 1) % 4]
                d_, f_, b_ = dec[n % 2], hf[n % 2], hb[n % 2]
                n += 1
                S.dma("sp", d_.t[:], env.i_decay[lc * 128:(lc + 1) * 128, ch * 512:(ch + 1) * 512], writes=[d_.r])
                S.op("pe", f_mms([(pf.t[:], h2.t[:, lc * 128:(lc + 1) * 128], w3.t[:, cf:cf + 512], True, True)]),
                     reads=[h2.r, w3.r], writes=[pf.r])
                S.op("pe", f_mms([(pb.t[:], h2.t[:, lc * 128:(lc + 1) * 128], w3.t[:, cb:cb + 512], True, True)]),
                     reads=[h2.r, w3.r], writes=[pb.r])
                S.op("dve", f_tt(f_.t[:], pf.t[:], d_.t[:], ALU.mult), reads=[pf.r, d_.r], writes=[f_.r])
                S.op("dve", f_tt(b_.t[:], pb.t[:], d_.t[:], ALU.mult), reads=[pb.r, d_.r], writes=[b_.r])
                if lc == 0:
                    S.op("dve", f_memset(b_.t[0:1, :], 0.0), writes=[b_.r])
                    S.op("pool", f_memset(acc.t[:], 0.0), writes=[acc.r])
                ta, tb = tmp[0], tmp[1]
                S.op("act", f_act(ta.t[:], f_.t[:], AF.Abs), reads=[f_.r], writes=[ta.r])
                S.op("act", f_act(tb.t[:], b_.t[:], AF.Abs), reads=[b_.r], writes=[tb.r])
                S.op("pool", f_tt(acc.t[:], acc.t[:], ta.t[:], ALU.add), reads=[ta.r, acc.r], writes=[acc.r])
                S.op("pool", f_tt(acc.t[:], acc.t[:], tb.t[:], ALU.add), reads=[tb.r, acc.r], writes=[acc.r])
                S.op("dve", f_tt(ee.t[:, lc, :], f_.t[:], b_.t[:], ALU.add), reads=[f_.r, b_.r], pwrites=[ee.r])
                S.op("pool", f_tt(dd.t[:, lc, :], f_.t[:], b_.t[:], ALU.subtract), reads=[f_.r, b_.r], pwrites=[dd.r])
            if env.dbg and l == 0 and ct == 0:
                S.dma("sp", env.dbg_e, ee.t[:], reads=[ee.r])
            items = [(psn.t[:, j:j + 1], acc.t[:, j * 128:(j + 1) * 128], env.ones_f.t[:, 0:1], True, True) for j in range(4)]
            S.op("pe", f_mms(items), reads=[acc.r, env.ones_f.r], writes=[psn.r])
            c0 = o * 8 + ch * 4
            S.op("dve", (lambda e, c0=c0: e.reciprocal(out=env.rn.t[:, l, c0:c0 + 4], in_=psn.t[:, 0:4])),
                 reads=[psn.r], writes=[env.rn.r])
            S.op("pe", f_mms([(psn.t[0:1, :], alt.t[:, 0:1], ee.t[:, lc, :], lc == 0, lc == 31) for lc in range(32)]),
                 reads=[alt.r, ee.r], writes=[psn.r])
            S.op("act", f_copy("act", nyq.t[:], psn.t[0:1, :]), reads=[psn.r], writes=[nyq.r])
            for fc in range(64):
                ft = Ft[nf % 3]
                nf += 1
                S.dma("sp", ft.t[:], env.i_Fh[fc].rearrange("p (t f) -> p t f", f=128), writes=[ft.r])
                src = ee if fc < 32 else dd
                p = ps[4 + (nk % 2)]
                kk = ko[nk % 2]
                nk += 1
                S.op("pe", f_mms([(p.t[:], ft.t[:, tc, :], src.t[:, tc, :], tc == 0, tc == 31) for tc in range(32)]),
                     reads=[ft.r, src.r], writes=[p.r])
                eng = "act" if fc % 2 else "dve"
                S.op(eng, f_copy(eng, kk.t[:], p.t[:]), reads=[p.r], writes=[kk.r])
                if fc == 32:
                    S.op("dve", f_copy("dve", kk.t[0:1, :], nyq.t[:]), reads=[nyq.r], writes=[kk.r])
                S.dma("sp", env.Kd[fc // 32, (fc % 32) * 128:(fc % 32 + 1) * 128, ct * 512:(ct + 1) * 512], kk.t[:],
                      reads=[kk.r])
        S.barrier()


def norm_subtile(S, nc, env, k, bufs, xTs, t0, ntok, A, B, hT_ap, hT_res, out_f32=None):
    x, sq, pss, rstd, tmp = bufs
    xv = xTs.rearrange("(dc p) t -> p dc t", p=128)
    S.dma("sp", x.t[:, :, 0:ntok], xv[:, :, t0:t0 + ntok], writes=[x.r])
    S.op("act", f_act(sq.t[:, :, 0:ntok], x.t[:, :, 0:ntok], AF.Square), reads=[x.r], writes=[sq.r])
    S.op("pe", f_mms([(pss.t[:, 0:ntok], env.ones_b.t[:], sq.t[:, dc, 0:ntok], dc == 0, dc == 15) for dc in range(16)]),
         reads=[sq.r, env.ones_b.r], writes=[pss.r])
    S.op("act", f_act(rstd.t[:, 0:ntok], pss.t[:, 0:ntok], AF.Sqrt, bias=env.epsb.t[:], scale=1.0 / D),
         reads=[pss.r, env.epsb.r], writes=[rstd.r])
    S.op("dve", (lambda e: e.reciprocal(out=rstd.t[:, 0:ntok], in_=rstd.t[:, 0:ntok])), reads=[rstd.r], writes=[rstd.r])
    for dc in range(16):
        t = tmp[dc % 2]
        if out_f32 is None:
            S.op("dve", f_stt(t.t[:, 0:ntok], x.t[:, dc, 0:ntok], A[:, dc:dc + 1], rstd.t[:, 0:ntok], ALU.mult, ALU.mult),
                 reads=[x.r, rstd.r, env.modT.r], writes=[t.r])
            S.op("act", f_act(hT_ap(dc), t.t[:, 0:ntok], AF.Identity, bias=B[:, dc:dc + 1], scale=1.0),
                 reads=[t.r, env.modT.r], pwrites=[hT_res])
        else:
            S.op("dve", f_stt(out_f32(dc), x.t[:, dc, 0:ntok], A[:, dc:dc + 1], rstd.t[:, 0:ntok], ALU.mult, ALU.mult),
                 reads=[x.r, rstd.r], pwrites=[hT_res])


def phase_proj(S, nc, env, s, l, which):
    TT = 1024
    SUB = 256
    m = env.modT
    if which == "in":
        W = env.i_w_in[l]
        nch = INC // 128
        A, B = m.t[:, l, 16:32, s], m.t[:, l, 0:16, s]
    else:
        W = env.i_ffn_up[l]
        nch = 2 * DFF // 128
        A, B = m.t[:, l, 64:80, s], m.t[:, l, 48:64, s]
    Wv = W.rearrange("(kc p) n -> p kc n", p=128)
    with contextlib.ExitStack() as k:
        hT = [Tile(k, nc, [128, 16, TT], BF16) for _ in range(2)]
        x = Tile(k, nc, [128, 16, SUB], F32)
        sq = Tile(k, nc, [128, 16, SUB], BF16)
        rstd = Tile(k, nc, [128, SUB], F32)
        tmp = [Tile(k, nc, [128, SUB], F32) for _ in range(2)]
        pss = Tile(k, nc, [128, 512], F32, psum=True)
        wt = [Tile(k, nc, [128, 16, 512], BF16) for _ in range(3)]
        ob = [Tile(k, nc, [128, TT], BF16) for _ in range(3)]
        ps = [[Tile(k, nc, [128, 512], F32, psum=True) for _ in range(2)] for _ in range(3)]
        bufs = (x, sq, pss, rstd, tmp)
        ntile = L // TT

        def do_norm(tt):
            h = hT[tt % 2]
            for sb in range(TT // SUB):
                t0 = tt * TT + sb * SUB
                norm_subtile(S, nc, env, k, bufs, env.xT[s], t0, SUB, A, B,
                             (lambda dc, h=h, sb=sb: h.t[:, dc, sb * SUB:(sb + 1) * SUB]), h.r)
        do_norm(0)
        nw = 0
        no = 0
        for tt in range(ntile):
            h = hT[tt % 2]
            for j in range(nch):
                if j % 4 == 0:
                    w = wt[nw % 3]
                    nw += 1
                    S.dma("pool", w.t[:], Wv[:, :, j * 128:(j + 4) * 128], writes=[w.r])
                pp = ps[no % 3]
                o = ob[no % 3]
                no += 1
                jj = j % 4
                for hf in range(2):
                    S.op("pe", f_mms([(pp[hf].t[:], w.t[:, kc, jj * 128:(jj + 1) * 128], h.t[:, kc, hf * 512:(hf + 1) * 512],
                                       kc == 0, kc == 15) for kc in range(16)]),
                         reads=[w.r, h.r], writes=[pp[hf].r])
                for hf in range(2):
                    oap = o.t[:, hf * 512:(hf + 1) * 512]
                    if which == "in" and j >= 60:
                        S.op("act", f_act(oap, pp[hf].t[:], AF.Sigmoid, bias=env.bgate.t[:, l, j - 60:j - 59], scale=1.0),
                             reads=[pp[hf].r, env.bgate.r], pwrites=[o.r])
                    else:
                        eng = "dve" if (hf + j) % 2 == 0 else "act"
                        S.op(eng, f_copy(eng, oap, pp[hf].t[:]), reads=[pp[hf].r], pwrites=[o.r])
                if which == "in":
                    dst = env.uT[s][j * 128:(j + 1) * 128, tt * TT:(tt + 1) * TT]
                elif j < 44:
                    dst = env.aT[s][j * 128:(j + 1) * 128, tt * TT:(tt + 1) * TT]
                else:
                    dst = env.gT[s][(j - 44) * 128:(j - 43) * 128, tt * TT:(tt + 1) * TT]
                S.dma("sp", dst, o.t[:], reads=[o.r])
                if j == nch // 2 and tt + 1 < ntile:
                    do_norm(tt + 1)
        S.barrier()


def phase_dwconv(S, nc, env, s, l):
    with contextlib.ExitStack() as k:
        u = [Tile(k, nc, [128, L], BF16) for _ in range(2)]
        o = [Tile(k, nc, [128, L], F32) for _ in range(2)]
        ob = [Tile(k, nc, [128, L], BF16) for _ in range(2)]
        for rc in range(24):
            uu, oo, bb = u[rc % 2], o[rc % 2], ob[rc % 2]
            w = env.hcw.t
            S.dma("sp", uu.t[:], env.uT[s][rc * 128:(rc + 1) * 128, :], writes=[uu.r])
            S.op("act", f_act(oo.t[:], uu.t[:], AF.Identity, bias=env.hcb.t[:, l, rc:rc + 1], scale=w[:, l, rc, 1:2]),
                 reads=[uu.r, env.hcw.r, env.hcb.r], writes=[oo.r])
            S.op("dve", f_stt(oo.t[:, 1:L], uu.t[:, 0:L - 1], w[:, l, rc, 0:1], oo.t[:, 1:L], ALU.mult, ALU.add),
                 reads=[uu.r, oo.r], writes=[oo.r])
            S.op("dve", f_stt(bb.t[:, 0:L - 1], uu.t[:, 1:L], w[:, l, rc, 2:3], oo.t[:, 0:L - 1], ALU.mult, ALU.add),
                 reads=[uu.r, oo.r], writes=[bb.r])
            S.op("pool", f_copy("pool", bb.t[:, L - 1:L], oo.t[:, L - 1:L]), reads=[oo.r, bb.r], writes=[bb.r])
            S.dma("sp", env.ucT[s][rc * 128:(rc + 1) * 128, :], bb.t[:], reads=[bb.r])
        S.barrier()


def phase_hyena(S, nc, env, s, l, hh, o):
    c0 = hh * 512
    zsrc = env.ucT[s][0:1024] if o == 0 else env.zmid[s]
    zdst = env.zmid[s] if o == 0 else env.zfT[s]
    xo = env.ucT[s][1024 * (o + 1):1024 * (o + 2)]
    with contextlib.ExitStack() as k:
        zc = Tile(k, nc, [128, L], BF16)
        zT = Tile(k, nc, [128, 32, 512], BF16)
        Y = Tile(k, nc, [128, 64, 512], BF16)
        Y0 = Res()
        Ft = [Tile(k, nc, [128, 32, 128], BF16) for _ in range(2)]
        Kt = [Tile(k, nc, [128, 2, 512], F32) for _ in range(2)]
        Xs = [Tile(k, nc, [128, 2, 512], F32) for _ in range(2)]
        tmp = [Tile(k, nc, [128, 512], F32) for _ in range(4)]
        Gt = [Tile(k, nc, [128, 4, 512], BF16) for _ in range(2)]
        zt = [Tile(k, nc, [128, 512], BF16) for _ in range(2)]
        xt = [Tile(k, nc, [128, 512], BF16) for _ in range(2)]
        zb = [Tile(k, nc, [128, 512], F32) for _ in range(2)]
        a32 = [Tile(k, nc, [128, 512], F32) for _ in range(2)]
        zo = [Tile(k, nc, [128, 512], BF16) for _ in range(2)]
        pst = [Tile(k, nc, [128, 512], BF16, psum=True) for _ in range(2)]
        pX = [Tile(k, nc, [128, 512], F32, psum=True) for _ in range(4)]
        n = 0
        for cc in range(4):
            S.dma("sp", zc.t[:], zsrc[c0 + cc * 128:c0 + (cc + 1) * 128, :], writes=[zc.r])
            for tg in range(8):
                p = pst[n % 2]
                n += 1
                S.op("pe", f_trs([(p.t[:, j * 128:(j + 1) * 128], zc.t[:, (tg * 4 + j) * 128:(tg * 4 + j + 1) * 128])
                                  for j in range(4)], env.ident_b.t[:]),
                     reads=[zc.r, env.ident_b.r], writes=[p.r])
                eng = "act" if tg % 2 else "dve"
                S.op(eng, f_copy(eng, zT.t[:, tg * 4:(tg + 1) * 4, cc * 128:(cc + 1) * 128],
                                 p.t[:, 0:512].rearrange("p (j c) -> p j c", j=4)),
                     reads=[p.r], pwrites=[zT.r])
        nf = 0
        for fp in range(32):
            kt = Kt[fp % 2]
            xs = Xs[fp % 2]
            S.dma("sp", kt.t[:, 0, :], env.Kd[0, fp * 128:(fp + 1) * 128, o * 1024 + c0:o * 1024 + c0 + 512], pwrites=[kt.r])
            S.dma("sp", kt.t[:, 1, :], env.Kd[1, fp * 128:(fp + 1) * 128, o * 1024 + c0:o * 1024 + c0 + 512], pwrites=[kt.r])
            for part in range(2):
                ft = Ft[nf % 2]
                nf += 1
                fc = part * 32 + fp
                S.dma("sp", ft.t[:], env.i_Fh[fc].rearrange("p (t f) -> p t f", f=128), writes=[ft.r])
                p = pX[(fp % 2) * 2 + part]
                S.op("pe", f_mms([(p.t[:], ft.t[:, tc, :], zT.t[:, tc, :], tc == 0, tc == 31) for tc in range(32)]),
                     reads=[ft.r, zT.r], writes=[p.r])
                S.op("act", f_copy("act", xs.t[:, part, :], p.t[:]), reads=[p.r], pwrites=[xs.r])
            yres = [Y0] if fp == 0 else []
            S.op("dve", f_tt(tmp[0].t[:], xs.t[:, 0, :], kt.t[:, 0, :], ALU.mult), reads=[xs.r, kt.r], writes=[tmp[0].r])
            S.op("pool", f_tt(tmp[1].t[:], xs.t[:, 1, :], kt.t[:, 1, :], ALU.mult), reads=[xs.r, kt.r], writes=[tmp[1].r])
            S.op("dve", f_tt(Y.t[:, fp, :], tmp[0].t[:], tmp[1].t[:], ALU.subtract), reads=[tmp[0].r, tmp[1].r],
                 pwrites=[Y.r], writes=yres)
            S.op("pool", f_tt(tmp[2].t[:], xs.t[:, 0, :], kt.t[:, 1, :], ALU.mult), reads=[xs.r, kt.r], writes=[tmp[2].r])
            S.op("dve", f_tt(tmp[3].t[:], xs.t[:, 1, :], kt.t[:, 0, :], ALU.mult), reads=[xs.r, kt.r], writes=[tmp[3].r])
            S.op("pool", f_tt(Y.t[:, 32 + fp, :], tmp[2].t[:], tmp[3].t[:], ALU.add), reads=[tmp[2].r, tmp[3].r],
                 pwrites=[Y.r], writes=yres)
            if fp == 0:
                S.op("dve", f_tt(Y.t[0:1, 0, :], xs.t[0:1, 0, :], kt.t[0:1, 0, :], ALU.mult), reads=[xs.r, kt.r], writes=[Y0])
                S.op("dve", f_tt(Y.t[0:1, 32, :], xs.t[0:1, 1, :], kt.t[0:1, 1, :], ALU.mult), reads=[xs.r, kt.r], writes=[Y0])
        ng = 0
        ne = 0
        for tt in range(8):
            for g4 in range(16):
                gt = Gt[ng % 2]
                ng += 1
                S.dma("sp", gt.t[:], env.i_Gh[tt, g4].rearrange("p (j t) -> p j t", t=512), writes=[gt.r])
                items = []
                for j in range(4):
                    fk = g4 * 4 + j
                    for cc in range(4):
                        items.append((pX[cc].t[:], Y.t[:, fk, cc * 128:(cc + 1) * 128], gt.t[:, j, :], fk == 0, fk == 63))
                S.op("pe", f_mms(items), reads=[Y.r, Y0, gt.r], writes=[pX[c].r for c in range(4)])
            for cc in range(4):
                i = ne % 2
                ne += 1
                ci = hh * 4 + cc
                rows = slice(c0 + cc * 128, c0 + (cc + 1) * 128)
                cols = slice(tt * 512, (tt + 1) * 512)
                S.dma("sp", zt[i].t[:], zsrc[rows, cols], writes=[zt[i].r])
                S.dma("sp", xt[i].t[:], xo[rows, cols], writes=[xt[i].r])
                S.op("act", f_act(zb[i].t[:], zt[i].t[:], AF.Copy, scale=env.hbT.t[:, l, o, ci:ci + 1]),
                     reads=[zt[i].r, env.hbT.r], writes=[zb[i].r])
                S.op("dve", f_stt(a32[i].t[:], pX[cc].t[:], env.rn.t[:, l, o * 8 + ci:o * 8 + ci + 1], zb[i].t[:],
                                  ALU.mult, ALU.add),
                     reads=[pX[cc].r, env.rn.r, zb[i].r], writes=[a32[i].r])
                S.op("pool", f_tt(zo[i].t[:], a32[i].t[:], xt[i].t[:], ALU.mult), reads=[a32[i].r, xt[i].r], writes=[zo[i].r])
                S.dma("sp", zdst[rows, cols], zo[i].t[:], reads=[zo[i].r])
        S.barrier()


def phase_attn(S, nc, env, s, l):
    QOFF = 3 * HW
    with contextlib.ExitStack() as k:
        if ATTN_DBG["eb"]:
            eb = load_eb(S, nc, env, k)
        else:
            eb = Tile(k, nc, [128, 24, 2, 128], BF16)
            S.op("pool", f_memset(eb.t[:], 1.0), writes=[eb.r])
        raw = [Tile(k, nc, [128, L], BF16) for _ in range(3)]
        qd = Tile(k, nc, [128, L], BF16)
        kd1 = Tile(k, nc, [128, L + 128 * 16], BF16)
        vd1 = Tile(k, nc, [128, L + 128 * 16], BF16)
        kd = [kd1] * 3
        vd = [vd1] * 3
        Vt0 = Tile(k, nc, [128, 68, 128], BF16)
        Vt1 = Tile(k, nc, [128, 68, 128], BF16)
        Vt = [[Vt0, Vt1]] * 3
        Num = Tile(k, nc, [128, L], F32)
        Den = Tile(k, nc, [128, L], F32)
        yb = Tile(k, nc, [128, L], BF16)
        p32 = [Tile(k, nc, [128, 512], F32) for _ in range(2)]
        pbf = [Tile(k, nc, [128, 512], BF16) for _ in range(2)]
        pst = [Tile(k, nc, [128, 512], BF16, psum=True) for _ in range(2)]
        pS = [Tile(k, nc, [128, 512], F32, psum=True) for _ in range(2)]
        pO = [Tile(k, nc, [128, 512], F32, psum=True) for _ in range(2)]
        S.op("pool", f_memset(Vt0.t[:], 0.0), writes=[Vt0.r])
        S.op("pool", f_memset(Vt1.t[:], 0.0), writes=[Vt1.r])
        nt = 0
        nb = 0
        for hp in range(4):
            for g, (_, d) in enumerate(GROUPS):
                M = L // d
                MP = M + 128
                r0 = QOFF + g * 512 + hp * 128
                for i3, off in enumerate((0, AW, 2 * AW)):
                    S.dma("sp", raw[i3].t[:], env.uT[s][r0 + off:r0 + off + 128, :], writes=[raw[i3].r])
                qv = qd.t[:].rearrange("p (d m) -> p d m", d=d)
                kv = kd[g].t[:, 0:d * MP].rearrange("p (d m) -> p d m", d=d)
                vv = vd[g].t[:, 0:d * MP].rearrange("p (d m) -> p d m", d=d)
                S.op("pool", f_memset(kd[g].t[:, 0:d * MP], 0.0), writes=[kd[g].r])
                S.op("pool", f_memset(vd[g].t[:, 0:d * MP], 0.0), writes=[vd[g].r])
                if ATTN_DBG["deint"]:
                    S.op("dve", f_copy("dve", qv, raw[0].t[:].rearrange("p (m d) -> p d m", d=d)), reads=[raw[0].r], writes=[qd.r])
                    S.op("pool", f_copy("pool", kv[:, :, 64:64 + M], raw[1].t[:].rearrange("p (m d) -> p d m", d=d)),
                         reads=[raw[1].r], pwrites=[kd[g].r])
                    S.op("dve", f_copy("dve", vv[:, :, 64:64 + M], raw[2].t[:].rearrange("p (m d) -> p d m", d=d)),
                         reads=[raw[2].r], pwrites=[vd[g].r])
                nchunk = MP // 128
                tot = d * nchunk
                for t0 in range(0, tot if ATTN_DBG["vtr"] else 0, 4):
                    p = pst[nt % 2]
                    nt += 1
                    items = []
                    cnt = min(4, tot - t0)
                    for j in range(cnt):
                        r, i = divmod(t0 + j, nchunk)
                        items.append((p.t[:, j * 128:(j + 1) * 128], vv[:, r, i * 128:(i + 1) * 128]))
                    S.op("pe", f_trs(items, env.ident_b.t[:]), reads=[vd[g].r, env.ident_b.r], writes=[p.r])
                    pv = p.t[:, 0:512].rearrange("p (j c) -> p j c", j=4)
                    S.op("act", f_copy("act", Vt[g][0].t[:, t0:t0 + cnt, 0:64], pv[:, 0:cnt, 0:64]), reads=[p.r], pwrites=[Vt[g][0].r])
                    S.op("dve", f_copy("dve", Vt[g][1].t[:, t0:t0 + cnt, 64:128], pv[:, 0:cnt, 64:128]), reads=[p.r], pwrites=[Vt[g][1].r])
                nblk = M // 128
                for r in range(d if ATTN_DBG["blocks"] else 0):
                    for b2 in range(0, nblk, 2):
                        for h in range(2):
                            gh = g * 8 + hp * 2 + h
                            ps_ = pS[nb % 2]
                            pf_ = p32[nb % 2]
                            pb_ = pbf[nb % 2]
                            nb += 1
                            hs = slice(h * 64, (h + 1) * 64)
                            items = []
                            for qb in range(2):
                                b = b2 + qb
                                for c in range(2):
                                    items.append((ps_.t[:, (qb * 2 + c) * 128:(qb * 2 + c + 1) * 128],
                                                  kv[hs, r, (b + c) * 128:(b + c + 1) * 128],
                                                  qv[hs, r, b * 128:(b + 1) * 128], True, True))
                            S.op("pe", f_mms(items), reads=[kd[g].r, qd.r], writes=[ps_.r])
                            S.op("act", f_act(pf_.t[:], ps_.t[:], AF.Exp, scale=0.125), reads=[ps_.r], writes=[pf_.r])
                            for qb in range(2):
                                S.op("dve", f_tt(pb_.t[:, qb * 256:(qb + 1) * 256], pf_.t[:, qb * 256:(qb + 1) * 256],
                                                 eb.t[:, gh, :, :].rearrange("p c q -> p (c q)"), ALU.mult),
                                     reads=[pf_.r, eb.r], pwrites=[pb_.r])
                            if h == 0:
                                pb0 = pb_
                                continue
                            pbs = (pb0, pb_)
                            po_ = pO[(nb // 2) % 2]
                            mm = []
                            for qb in range(2):
                                b = b2 + qb
                                for isden in range(2):
                                    oreg = po_.t[:, isden * 256 + qb * 128:isden * 256 + (qb + 1) * 128]
                                    idx = 0
                                    for h2 in range(2):
                                        for c in range(2):
                                            vi = r * nchunk + b + c
                                            vsel = 1 if b + c == 0 else (2 if b + c == nchunk - 1 else 0)
                                            rhs = pbs[h2].t[:, (qb * 2 + c) * 128:(qb * 2 + c + 1) * 128]
                                            lhs = env.ov.t[:, 2 * vsel + h2, :] if isden else Vt[g][h2].t[:, vi, :]
                                            mm.append((oreg, lhs, rhs, idx == 0, idx == 3))
                                            idx += 1
                            S.op("pe", f_mms(mm), reads=[pb0.r, pb_.r, Vt[g][0].r, Vt[g][1].r, env.ov.r], writes=[po_.r])
                            t_lo = r + d * 128 * b2
                            sl = slice(t_lo, t_lo + d * 255 + 1, d) if d > 1 else slice(t_lo, t_lo + 256)
                            if g == 0:
                                S.op("dve", f_copy("dve", Num.t[:, sl], po_.t[:, 0:256]), reads=[po_.r], writes=[Num.r])
                                S.op("act", f_copy("act", Den.t[:, sl], po_.t[:, 256:512]), reads=[po_.r], writes=[Den.r])
                            else:
                                S.op("dve", f_tt(Num.t[:, sl], Num.t[:, sl], po_.t[:, 0:256], ALU.add),
                                     reads=[po_.r, Num.r], writes=[Num.r])
                                S.op("dve", f_tt(Den.t[:, sl], Den.t[:, sl], po_.t[:, 256:512], ALU.add),
                                     reads=[po_.r, Den.r], writes=[Den.r])
            if ATTN_DBG["tail"]:
                S.op("dve", (lambda e: e.reciprocal(out=Den.t[:], in_=Den.t[:])), reads=[Den.r], writes=[Den.r])
                S.op("pool", f_tt(yb.t[:], Num.t[:], Den.t[:], ALU.mult), reads=[Num.r, Den.r], writes=[yb.r])
                S.dma("sp", env.yatT[s][hp * 128:(hp + 1) * 128, :], yb.t[:], reads=[yb.r])
        S.barrier()


def residual_tile(S, nc, env, s, pp, dchunk, t0, TT, gcol, xr, xn):
    rows = slice(dchunk * 128, (dchunk + 1) * 128)
    S.dma("sp", xr.t[:], env.xT[s][rows, t0:t0 + TT], writes=[xr.r])
    for hf in range(TT // 512):
        S.op("dve", f_stt(xn.t[:, hf * 512:(hf + 1) * 512], pp[hf].t[:], gcol, xr.t[:, hf * 512:(hf + 1) * 512],
                          ALU.mult, ALU.add),
             reads=[pp[hf].r, xr.r, env.modT.r], pwrites=[xn.r])
    S.dma("sp", env.xT[s][rows, t0:t0 + TT], xn.t[:], reads=[xn.r])


def phase_merge(S, nc, env, s, l):
    TT = 1024
    GOFF = 3 * HW + 3 * AW
    m = env.modT
    with contextlib.ExitStack() as k:
        zf = Tile(k, nc, [128, 8, TT], BF16)
        ya = Tile(k, nc, [128, 4, TT], BF16)
        mg = Tile(k, nc, [128, 16, TT], BF16)
        wh = [Tile(k, nc, [128, 8, 512], BF16) for _ in range(2)]
        wa = [Tile(k, nc, [128, 4, 512], BF16) for _ in range(2)]
        wo = [Tile(k, nc, [128, 16, 512], BF16) for _ in range(2)]
        gh_ = [Tile(k, nc, [128, TT], BF16) for _ in range(2)]
        ga_ = [Tile(k, nc, [128, TT], BF16) for _ in range(2)]
        t1 = [Tile(k, nc, [128, TT], F32) for _ in range(2)]
        t2 = [Tile(k, nc, [128, TT], F32) for _ in range(2)]
        xr = [Tile(k, nc, [128, TT], F32) for _ in range(2)]
        xn = [Tile(k, nc, [128, TT], F32) for _ in range(2)]
        ps = [[Tile(k, nc, [128, 512], F32, psum=True) for _ in range(2)] for _ in range(4)]
        Whv = env.i_w_br_hy[l].rearrange("(kc p) n -> p kc n", p=128)
        Wav = env.i_w_br_attn[l].rearrange("(kc p) n -> p kc n", p=128)
        Wov = env.i_w_out[l].rearrange("(kc p) n -> p kc n", p=128)
        nn = 0
        for tt in range(L // TT):
            cols = slice(tt * TT, (tt + 1) * TT)
            S.dma("sp", zf.t[:], env.zfT[s].rearrange("(c p) t -> p c t", p=128)[:, :, cols], writes=[zf.r])
            S.dma("sp", ya.t[:], env.yatT[s].rearrange("(c p) t -> p c t", p=128)[:, :, cols], writes=[ya.r])
            for dj in range(16):
                i = dj % 2
                if dj % 4 == 0:
                    w_h, w_a = wh[(dj // 4) % 2], wa[(dj // 4) % 2]
                    S.dma("pool", w_h.t[:], Whv[:, :, dj * 128:(dj + 4) * 128], writes=[w_h.r])
                    S.dma("pool", w_a.t[:], Wav[:, :, dj * 128:(dj + 4) * 128], writes=[w_a.r])
                S.dma("sp", gh_[i].t[:], env.uT[s][GOFF + dj * 128:GOFF + (dj + 1) * 128, cols], writes=[gh_[i].r])
                S.dma("sp", ga_[i].t[:], env.uT[s][GOFF + D + dj * 128:GOFF + D + (dj + 1) * 128, cols], writes=[ga_[i].r])
                p1, p2 = ps[(2 * dj) % 4], ps[(2 * dj + 1) % 4]
                jj = dj % 4
                for hf in range(2):
                    S.op("pe", f_mms([(p1[hf].t[:], w_h.t[:, c, jj * 128:(jj + 1) * 128], zf.t[:, c, hf * 512:(hf + 1) * 512],
                                       c == 0, c == 7) for c in range(8)]), reads=[w_h.r, zf.r], writes=[p1[hf].r])
                    S.op("pe", f_mms([(p2[hf].t[:], w_a.t[:, c, jj * 128:(jj + 1) * 128], ya.t[:, c, hf * 512:(hf + 1) * 512],
                                       c == 0, c == 3) for c in range(4)]), reads=[w_a.r, ya.r], writes=[p2[hf].r])
                for hf in range(2):
                    sl = slice(hf * 512, (hf + 1) * 512)
                    S.op("dve", f_tt(t1[i].t[:, sl], p1[hf].t[:], gh_[i].t[:, sl], ALU.mult), reads=[p1[hf].r, gh_[i].r],
                         pwrites=[t1[i].r])
                    S.op("dve", f_tt(t2[i].t[:, sl], p2[hf].t[:], ga_[i].t[:, sl], ALU.mult), reads=[p2[hf].r, ga_[i].r],
                         pwrites=[t2[i].r])
                S.op("pool", f_tt(mg.t[:, dj, :], t1[i].t[:], t2[i].t[:], ALU.add), reads=[t1[i].r, t2[i].r], pwrites=[mg.r])
            for dp in range(16):
                if dp % 4 == 0:
                    w_o = wo[(dp // 4) % 2]
                    S.dma("pool", w_o.t[:], Wov[:, :, dp * 128:(dp + 4) * 128], writes=[w_o.r])
                pp = ps[dp % 4]
                jj = dp % 4
                for hf in range(2):
                    S.op("pe", f_mms([(pp[hf].t[:], w_o.t[:, c, jj * 128:(jj + 1) * 128], mg.t[:, c, hf * 512:(hf + 1) * 512],
                                       c == 0, c == 15) for c in range(16)]), reads=[w_o.r, mg.r], writes=[pp[hf].r])
                residual_tile(S, nc, env, s, pp, dp, tt * TT, TT, m.t[:, l, 32 + dp, s:s + 1], xr[nn % 2], xn[nn % 2])
                nn += 1
        S.barrier()


def phase_ffn2(S, nc, env, s, l):
    TT = 1024
    m = env.modT
    NF = DFF // 128
    with contextlib.ExitStack() as k:
        hid = Tile(k, nc, [128, NF, TT], BF16)
        gt = [Tile(k, nc, [128, TT + 2], BF16) for _ in range(2)]
        at = [Tile(k, nc, [128, TT], BF16) for _ in range(2)]
        o32 = [Tile(k, nc, [128, TT], F32) for _ in range(2)]
        s32 = [Tile(k, nc, [128, TT], F32) for _ in range(2)]
        wd = [Tile(k, nc, [128, NF, 128], BF16) for _ in range(2)]
        xr = [Tile(k, nc, [128, TT], F32) for _ in range(2)]
        xn = [Tile(k, nc, [128, TT], F32) for _ in range(2)]
        ps = [[Tile(k, nc, [128, 512], F32, psum=True) for _ in range(2)] for _ in range(3)]
        Wdv = env.i_ffn_down[l].rearrange("(kc p) n -> p kc n", p=128)
        w = env.fcw.t
        nn = 0
        for tt in range(L // TT):
            t0 = tt * TT
            for fk in range(NF):
                i = fk % 2
                g_, a_, o_, s_ = gt[i], at[i], o32[i], s32[i]
                lo = max(t0 - 1, 0)
                hi = min(t0 + TT + 1, L)
                S.op("pool", f_memset(g_.t[:, 0:1], 0.0), writes=[g_.r])
                S.op("pool", f_memset(g_.t[:, TT + 1:TT + 2], 0.0), writes=[g_.r])
                S.dma("sp", g_.t[:, lo - (t0 - 1):hi - (t0 - 1)], env.gT[s][fk * 128:(fk + 1) * 128, lo:hi], writes=[g_.r])
                S.dma("sp", a_.t[:], env.aT[s][fk * 128:(fk + 1) * 128, t0:t0 + TT], writes=[a_.r])
                S.op("act", f_act(o_.t[:], g_.t[:, 1:TT + 1], AF.Identity, bias=env.fcb.t[:, l, fk:fk + 1], scale=w[:, l, fk, 1:2]),
                     reads=[g_.r, env.fcw.r, env.fcb.r], writes=[o_.r])
                S.op("dve", f_stt(o_.t[:], g_.t[:, 0:TT], w[:, l, fk, 0:1], o_.t[:], ALU.mult, ALU.add), reads=[g_.r, o_.r], writes=[o_.r])
                S.op("dve", f_stt(o_.t[:], g_.t[:, 2:TT + 2], w[:, l, fk, 2:3], o_.t[:], ALU.mult, ALU.add), reads=[g_.r, o_.r], writes=[o_.r])
                S.op("act", f_act(s_.t[:], o_.t[:], AF.Silu), reads=[o_.r], writes=[s_.r])
                S.op("pool", f_tt(hid.t[:, fk, :], s_.t[:], a_.t[:], ALU.mult), reads=[s_.r, a_.r], pwrites=[hid.r])
            for dp in range(16):
                wt = wd[dp % 2]
                S.dma("pool", wt.t[:], Wdv[:, :, dp * 128:(dp + 1) * 128], writes=[wt.r])
                pp = ps[dp % 3]
                for hf in range(2):
                    S.op("pe", f_mms([(pp[hf].t[:], wt.t[:, c, :], hid.t[:, c, hf * 512:(hf + 1) * 512], c == 0, c == NF - 1)
                                      for c in range(NF)]), reads=[wt.r, hid.r], writes=[pp[hf].r])
                residual_tile(S, nc, env, s, pp, dp, t0, TT, m.t[:, l, 80 + dp, s:s + 1], xr[nn % 2], xn[nn % 2])
                nn += 1
        S.barrier()


def phase_final(S, nc, env, s):
    SUB = 256
    with contextlib.ExitStack() as k:
        x = Tile(k, nc, [128, 16, SUB], F32)
        sq = Tile(k, nc, [128, 16, SUB], BF16)
        rstd = Tile(k, nc, [128, SUB], F32)
        tmp = [Tile(k, nc, [128, SUB], F32) for _ in range(2)]
        pss = Tile(k, nc, [128, 512], F32, psum=True)
        yT = Tile(k, nc, [128, 16, SUB], F32)
        yo = [Tile(k, nc, [128, D], F32) for _ in range(2)]
        ps = [Tile(k, nc, [128, 512], F32, psum=True) for _ in range(4)]
        bufs = (x, sq, pss, rstd, tmp)
        n = 0
        for sb in range(L // SUB):
            t0 = sb * SUB
            norm_subtile(S, nc, env, k, bufs, env.xT[s], t0, SUB, env.fing.t[:], None, None, yT.r,
                         out_f32=(lambda dc: yT.t[:, dc, :]))
            for q in range(SUB // 128):
                o = yo[(sb * 2 + q) % 2]
                for g4 in range(4):
                    p = ps[n % 4]
                    n += 1
                    S.op("pe", f_trs([(p.t[:, j * 128:(j + 1) * 128], yT.t[:, g4 * 4 + j, q * 128:(q + 1) * 128]) for j in range(4)],
                                     env.ident_f.t[:]), reads=[yT.r, env.ident_f.r], writes=[p.r])
                    eng = "act" if g4 % 2 else "dve"
                    S.op(eng, f_copy(eng, o.t[:, g4 * 512:(g4 + 1) * 512], p.t[:]), reads=[p.r], pwrites=[o.r])
                S.dma("sp", env.yout[s, t0 + q * 128:t0 + (q + 1) * 128, :], o.t[:], reads=[o.r])
        S.barrier()


def build(nseq=2, depth=DEPTH, dbg=False, stop_after=None, phases=None, ext_in=()):
    nc = bass.Bass("TRN2", target_bir_lowering=False)
    env = Env()
    env.nseq, env.depth = nseq, depth
    env.dbg = dbg

    def din(name, shape, dt=F32):
        return nc.dram_tensor(name, list(shape), dt, kind="ExternalInput").ap()

    def dscr(name, shape, dt):
        kind = "ExternalInput" if name in ext_in else ("ExternalOutput" if dbg else "Internal")
        h = nc.dram_tensor(name, list(shape), dt, kind=kind)
        return h

    def want(ph):
        return phases is None or ph in phases

    env.xin = din("xin", [nseq, L, D])
    env.i_cT = din("cT", [128, 16, nseq])
    env.i_ada_w = din("ada_w", [depth, D, 6 * D])
    env.i_adab = din("adab", [128, depth, 96])
    env.i_n1g = din("n1g", [128, depth, 16])
    env.i_n2g = din("n2g", [128, depth, 16])
    env.i_fing = din("fing", [128, 16])
    env.i_w_in = din("w_in", [depth, D, INC])
    env.i_bgate = din("bgate", [128, depth, 32])
    env.i_hcw = din("hcw", [128, depth, 24, 3])
    env.i_hcb = din("hcb", [128, depth, 24])
    env.i_fw1 = din("fw1", [depth, 33, 64])
    env.i_fw2 = din("fw2", [depth, 64, 64])
    env.i_fw3 = din("fw3", [depth, 64, 4096])
    env.i_fpar = din("fpar", [depth, 64, 4])
    env.i_hbT = din("hbT", [128, depth, 2, 8])
    env.i_rel_bias = din("rel_bias", [32, 24])
    env.i_w_br_hy = din("w_br_hy", [depth, HW, D])
    env.i_w_br_attn = din("w_br_attn", [depth, 512, D])
    env.i_w_out = din("w_out", [depth, D, D])
    env.i_ffn_up = din("ffn_up", [depth, D, 2 * DFF])
    env.i_fcw = din("fcw", [128, depth, 44, 3])
    env.i_fcb = din("fcb", [128, depth, 44])
    env.i_ffn_down = din("ffn_down", [depth, DFF, D])
    env.i_Fh = din("Fh", [64, 128, 32 * 128], BF16)
    env.i_Gh = din("Gh", [8, 16, 128, 4 * 512], BF16)
    env.i_featT = din("featT", [33, L])
    env.i_decay = din("decay", [L, HW])
    env.i_selw = din("selw", [32, 3, WLEN])
    env.i_ident_f = din("ident_f", [128, 128])
    env.i_ident_b = din("ident_b", [128, 128], BF16)
    env.i_alt = din("alt", [128, 2], BF16)
    env.yout = nc.dram_tensor("yout", [nseq, L, D], F32, kind="ExternalOutput").ap()
    env.xT = dscr("xT", [nseq, D, L], F32).ap()
    env.uT = dscr("uT", [nseq, INC, L], BF16).ap()
    env.ucT = dscr("ucT", [nseq, 3 * HW, L], BF16).ap()
    env.zmid = dscr("zmid", [nseq, HW, L], BF16).ap()
    env.zfT = dscr("zfT", [nseq, HW, L], BF16).ap()
    env.yatT = dscr("yatT", [nseq, 512, L], BF16).ap()
    env.aT = dscr("aT", [nseq, DFF, L], BF16).ap()
    env.gT = dscr("gT", [nseq, DFF, L], BF16).ap()
    env.Kd = dscr("Kd", [2, L, 2 * HW], F32).ap()
    env.Wd_h = dscr("Wd", [24, 128, WLEN], F32)
    if dbg:
        env.dbg_h1 = dscr("dbg_h1", [64, L], F32).ap()
        env.dbg_h2 = dscr("dbg_h2", [64, L], F32).ap()
        env.dbg_e = dscr("dbg_e", [128, 32, 512], BF16).ap()
    env.Wd = env.Wd_h.ap()

    with contextlib.ExitStack() as stk:
        env.pstk = stk
        S = Sched(nc, stk)

        def plan():
            phase_consts(S, nc, env)
            if want("xT"):
                for s in range(nseq):
                    phase_xT(S, nc, env, s)
            if want("mod"):
                phase_mod(S, nc, env)
            if want("eb1"):
                phase_eb1(S, nc, env)
            for l in range(depth):
                if want("filter"):
                    phase_filter(S, nc, env, l)
                for s in range(nseq):
                    if want("proj_in"):
                        phase_proj(S, nc, env, s, l, "in")
                    if stop_after == "proj":
                        return
                    if want("dwconv"):
                        phase_dwconv(S, nc, env, s, l)
                    if want("hyena"):
                        for o in range(2):
                            for hh in range(2):
                                phase_hyena(S, nc, env, s, l, hh, o)
                    if stop_after == "hyena":
                        return
                    if want("attn"):
                        phase_attn(S, nc, env, s, l)
                    if stop_after == "attn":
                        return
                    if want("merge"):
                        phase_merge(S, nc, env, s, l)
                    if stop_after == "merge":
                        return
                    if want("proj_up"):
                        phase_proj(S, nc, env, s, l, "up")
                    if want("ffn2"):
                        phase_ffn2(S, nc, env, s, l)
            if want("final"):
                for s in range(nseq):
                    phase_final(S, nc, env, s)
        plan()
        S.barrier()
        block = stk.enter_context(nc.Block())
        S.emit(block)
    return nc


def t5_bucket(rel):
    nb = 16
    ret = (rel > 0).astype(np.int32) * nb
    n = np.abs(rel)
    max_exact = nb // 2
    large = max_exact + (np.log(np.maximum(n, 1) / max_exact) / np.log(1024 / max_exact) * (nb - max_exact)).astype(np.int32)
    large = np.minimum(large, nb - 1)
    return ret + np.where(n < max_exact, n, large)


_CONST = {}


def host_consts():
    if _CONST:
        return _CONST
    bf = ml_dtypes.bfloat16
    t = np.arange(L, dtype=np.float64)[:, None]
    f = np.arange(L, dtype=np.float64)[None, :]
    ang = 2.0 * np.pi * ((t * f) % NFFT) / NFFT
    C = np.cos(ang)
    Sn = np.sin(ang)
    Sn[:, 0] = (-1.0) ** np.arange(L)
    Fm = np.concatenate([C, Sn], axis=1)
    Fh = Fm.reshape(32, 128, 64, 128).transpose(2, 1, 0, 3).reshape(64, 128, 32 * 128)
    Gc = C.T.copy()
    Gs = Sn.T.copy()
    Gc[0, :] *= 0.5
    Gs[0, :] *= 0.5
    Gm = np.concatenate([Gc, Gs], axis=0) * (2.0 / NFFT)
    Gh = Gm.reshape(16, 4, 128, 8, 512).transpose(3, 0, 2, 1, 4).reshape(8, 16, 128, 4 * 512)
    _CONST["Fh"] = np.ascontiguousarray(Fh).astype(bf)
    _CONST["Gh"] = np.ascontiguousarray(Gh).astype(bf)
    tt = np.linspace(0.0, 1.0, L, dtype=np.float32)[:, None]
    pos = np.arange(L, dtype=np.float32)[:, None]
    bands = np.linspace(1e-4, 15, 16, dtype=np.float32)[None, :]
    angf = (np.float32(2.0 * math.pi / L) * pos * bands).astype(np.float32)
    feat = np.concatenate([tt, np.cos(angf), -np.sin(angf)], axis=-1).astype(np.float32)
    _CONST["featT"] = np.ascontiguousarray(feat.T)
    max_decay = math.log(1e-2) / 0.3
    min_decay = math.log(1e-2) / 1.5
    deltas = np.abs(np.linspace(min_decay, max_decay, HW, dtype=np.float32))
    _CONST["decay"] = np.exp(-tt * deltas).astype(np.float32)
    selw = np.zeros((32, 3, WLEN), np.float32)
    for g, (_, dil) in enumerate(GROUPS):
        for i in range(WLEN):
            j = 191 - i
            if abs(j) <= 64:
                selw[int(t5_bucket(np.array([j * dil]))[0]), g, i] = 1.0
    _CONST["selw"] = selw
    _CONST["ident_f"] = np.eye(128, dtype=np.float32)
    _CONST["ident_b"] = np.eye(128, dtype=np.float32).astype(bf)
    alt = np.stack([(-1.0) ** np.arange(128), np.ones(128)], axis=1).astype(bf)
    _CONST["alt"] = alt
    return _CONST


def pcol(v, n):
    return np.ascontiguousarray(np.asarray(v, np.float32).reshape(n, 128).T)


def shared_inputs(p, depth):
    c = host_consts()
    m = dict(c)
    f32 = np.float32
    m["ada_w"] = np.ascontiguousarray(p["ada_w"][:depth], f32)
    m["adab"] = np.ascontiguousarray(np.stack([pcol(p["ada_b"][l], 96) for l in range(depth)], axis=1))
    m["n1g"] = np.ascontiguousarray(np.stack([pcol(p["norm1_g"][l], 16) for l in range(depth)], axis=1))
    m["n2g"] = np.ascontiguousarray(np.stack([pcol(p["norm2_g"][l], 16) for l in range(depth)], axis=1))
    m["fing"] = pcol(p["final_g"], 16)
    m["w_in"] = np.ascontiguousarray(p["w_in"][:depth], f32)
    m["bgate"] = np.ascontiguousarray(np.stack([pcol(p["b_gate"][l], 32) for l in range(depth)], axis=1))
    m["hcw"] = np.ascontiguousarray(np.stack(
        [np.asarray(p["hy_conv_w"][l], f32).T.reshape(24, 128, 3).transpose(1, 0, 2) for l in range(depth)], axis=1))
    m["hcb"] = np.ascontiguousarray(np.stack([pcol(p["hy_conv_b"][l], 24) for l in range(depth)], axis=1))
    m["fw1"] = np.ascontiguousarray(p["filt_w1"][:depth], f32)
    m["fw2"] = np.ascontiguousarray(p["filt_w2"][:depth], f32)
    m["fw3"] = np.ascontiguousarray(p["filt_w3"][:depth], f32)
    m["fpar"] = np.ascontiguousarray(np.stack(
        [np.stack([p["filt_b1"][l], p["filt_freq1"][l], p["filt_b2"][l], p["filt_freq2"][l]], axis=1) for l in range(depth)],
        axis=0).astype(f32))
    m["hbT"] = np.ascontiguousarray(np.stack(
        [np.asarray(p["hy_bias"][l], f32).reshape(2, 8, 128).transpose(2, 0, 1) for l in range(depth)], axis=1))
    m["rel_bias"] = np.ascontiguousarray(p["rel_bias"], f32)
    m["w_br_hy"] = np.ascontiguousarray(p["w_br_hy"][:depth], f32)
    m["w_br_attn"] = np.ascontiguousarray(p["w_br_attn"][:depth], f32)
    m["w_out"] = np.ascontiguousarray(p["w_out"][:depth], f32)
    m["ffn_up"] = np.ascontiguousarray(p["ffn_up"][:depth], f32)
    m["fcw"] = np.ascontiguousarray(np.stack(
        [np.asarray(p["ffn_conv_w"][l], f32).T.reshape(44, 128, 3).transpose(1, 0, 2) for l in range(depth)], axis=1))
    m["fcb"] = np.ascontiguousarray(np.stack([pcol(p["ffn_conv_b"][l], 44) for l in range(depth)], axis=1))
    m["ffn_down"] = np.ascontiguousarray(p["ffn_down"][:depth], f32)
    return m


def core_inputs(shared, xs, cs):
    m = dict(shared)
    m["xin"] = np.ascontiguousarray(np.stack(xs, axis=0), np.float32)
    cT = np.stack([np.asarray(c, np.float32).reshape(16, 128).T for c in cs], axis=2)
    m["cT"] = np.ascontiguousarray(cT)
    return m


_NC_CACHE = {}


def kernel(x_prompt, x_sample, c_prompt, c_sample, **p):
    x_prompt = np.asarray(x_prompt)
    x_sample = np.asarray(x_sample)
    c_prompt = np.asarray(c_prompt)
    c_sample = np.asarray(c_sample)
    p = {k_: np.asarray(v) for k_, v in p.items()}
    if "nc" not in _NC_CACHE:
        _NC_CACHE["nc"] = build(2, DEPTH)
    nc = _NC_CACHE["nc"]
    shared = shared_inputs(p, DEPTH)
    in_maps = []
    for i in range(8):
        in_maps.append(core_inputs(shared, [x_sample[i], x_prompt[i % 4]], [c_sample[i], c_prompt[i % 4]]))
    res = run_bass_kernel_spmd(nc, in_maps, core_ids=list(range(8)))
    y_sample = np.stack([res.results[i]["yout"][0] for i in range(8)], axis=0).astype(np.float32)
    y_prompt = np.stack([res.results[i]["yout"][1] for i in range(4)], axis=0).astype(np.float32)
    return (y_prompt, y_sample)
```

```python
import math
import contextlib
import numpy as np
import ml_dtypes
import concourse.bass as bass
import concourse.mybir as mybir
from concourse.bass_utils import run_bass_kernel_spmd

F32 = mybir.dt.float32
BF16 = mybir.dt.bfloat16
AF = mybir.ActivationFunctionType
ALU = mybir.AluOpType

D = 2048
L = 4096
DEPTH = 4
HW = 1024
AW = 1536
INC = 3 * HW + 3 * AW + 2 * D
DFF = 5632
NFFT = 2 * L
GROUPS = ((128, 1), (512, 4), (2048, 16))
EPS = 1e-6
WLEN = 383


class Res:
    __slots__ = ("w", "r", "base", "excl")

    def __init__(self):
        self.w = {}
        self.r = {}
        self.base = {}
        self.excl = False


class Tile:
    def __init__(self, stk, nc, shape, dt, psum=False):
        if psum:
            if dt == BF16 and list(shape) == [128, 512]:
                shape = [128, 1024]
            self.t = stk.enter_context(nc.psum_tensor(list(shape), dt))
        else:
            self.t = stk.enter_context(nc.sbuf_tensor(list(shape), dt))
        self.r = Res()
        self.r.excl = psum


class Sched:
    ENG = ("pe", "act", "dve", "pool", "sp")
    NP = 16

    def __init__(self, nc, stk):
        self.nc = nc
        self.ops = {e: [] for e in self.ENG}
        self.csem = {e: stk.enter_context(nc.semaphore("c_" + e)) for e in ("pe", "act", "dve", "pool")}
        self.cnt = {e: 0 for e in self.csem}
        self.dsem = {q: [stk.enter_context(nc.semaphore("d_%s%d" % (q, i))) for i in range(self.NP)]
                     for q in ("sp", "pool")}
        self.duse = {q: [0] * self.NP for q in ("sp", "pool")}
        self.dnext = {q: 0 for q in ("sp", "pool")}
        self.waited = {e: {} for e in self.ENG}
        self.bar = stk.enter_context(nc.semaphore("bar"))
        self.nbar = 0

    def _deps(self, reads, writes, pwrites):
        deps = {}

        def add(d):
            for s, v in d.items():
                if deps.get(s, 0) < v:
                    deps[s] = v
        for x in reads:
            add(x.w)
            if x.excl:
                add(x.r)
        for x in writes:
            add(x.w)
            add(x.r)
        for x in pwrites:
            if x.r:
                x.base = x.r
                x.r = {}
                x.w = {}
            add(x.base)
        return deps

    def _commit(self, tok, reads, writes, pwrites):
        s, v = tok
        for x in reads:
            if x.r.get(s, 0) < v:
                x.r[s] = v
        for x in writes:
            x.w = {s: v}
            x.r = {}
            x.base = {s: v}
        for x in pwrites:
            if x.w.get(s, 0) < v:
                x.w[s] = v

    def _waits(self, eng, deps):
        wd = self.waited[eng]
        out = []
        for s, v in deps.items():
            if wd.get(s, 0) < v:
                wd[s] = v
                out.append((s, v))
        return out

    def op(self, eng, fn, reads=(), writes=(), pwrites=()):
        deps = self._deps(reads, writes, pwrites)
        if eng == "pe":
            deps.pop(self.csem["pe"], None)
        waits = self._waits(eng, deps)
        self.cnt[eng] += 1
        tok = (self.csem[eng], self.cnt[eng])
        self.ops[eng].append((waits, fn, tok[0], 1))
        self._commit(tok, reads, writes, pwrites)

    def dma(self, q, out, in_, reads=(), writes=(), pwrites=()):
        deps = self._deps(reads, writes, pwrites)
        i = self.dnext[q]
        self.dnext[q] = (i + 1) % self.NP
        sem = self.dsem[q][i]
        if self.duse[q][i] > 0:
            v = 16 * self.duse[q][i]
            if deps.get(sem, 0) < v:
                deps[sem] = v
        waits = self._waits(q, deps)
        self.duse[q][i] += 1
        tok = (sem, 16 * self.duse[q][i])
        self.ops[q].append((waits, (lambda e, o=out, i_=in_: e.dma_start(out=o, in_=i_)), sem, 16))
        self._commit(tok, reads, writes, pwrites)

    def barrier(self):
        deps = {}
        for e, s in self.csem.items():
            if self.cnt[e]:
                deps[s] = self.cnt[e]
        for q in self.dsem:
            for i, s in enumerate(self.dsem[q]):
                if self.duse[q][i]:
                    deps[s] = 16 * self.duse[q][i]
        waits = self._waits("sp", dict(deps))
        self.nbar += 1
        bar, n = self.bar, self.nbar
        self.ops["sp"].append((waits, (lambda e: e.sem_inc(bar, 1)), None, 0))
        for e in ("pe", "act", "dve", "pool"):
            self.ops[e].append(([(bar, n)], None, None, 0))
            wd = self.waited[e]
            for s, v in deps.items():
                if wd.get(s, 0) < v:
                    wd[s] = v

    def emit(self, block):
        nc = self.nc

        def run(eng_name):
            def body(e):
                for waits, fn, sem, amt in self.ops[eng_name]:
                    for s, v in waits:
                        e.wait_ge(s, v)
                    if fn is not None:
                        ins = fn(e)
                        if sem is not None:
                            ins.then_inc(sem, amt)
            return body
        block.tensor(run("pe"))
        block.scalar(run("act"))
        block.vector(run("dve"))
        block.gpsimd(run("pool"))
        block.sync(run("sp"))


def f_copy(eng, out, in_):
    if eng == "act":
        return lambda e: e.activation(out=out, in_=in_, func=AF.Copy)
    return lambda e: e.tensor_copy(out=out, in_=in_)


def f_act(out, in_, func, bias=None, scale=None):
    kw = {}
    if bias is not None:
        kw["bias"] = bias
    if scale is not None:
        kw["scale"] = scale
    return lambda e: e.activation(out=out, in_=in_, func=func, **kw)


def f_tt(out, a, b, op):
    return lambda e: e.tensor_tensor(out=out, in0=a, in1=b, op=op)


def f_ts(out, a, s1, s2, op0, op1=None):
    if op1 is None:
        return lambda e: e.tensor_scalar(out=out, in0=a, scalar1=s1, scalar2=None, op0=op0)
    return lambda e: e.tensor_scalar(out=out, in0=a, scalar1=s1, scalar2=s2, op0=op0, op1=op1)


def f_stt(out, in0, scalar, in1, op0, op1):
    return lambda e: e.scalar_tensor_tensor(out=out, in0=in0, scalar=scalar, in1=in1, op0=op0, op1=op1)


def f_mms(items):
    def fn(e):
        ins = None
        for (o, l, r, a, b) in items:
            ins = e.matmul(o, l, r, start=a, stop=b)
        return ins
    return fn


def f_trs(items, ident):
    def fn(e):
        ins = None
        for (o, i) in items:
            ins = e.transpose(o, i, ident)
        return ins
    return fn


def f_memset(ap, v):
    return lambda e: e.memset(ap, v)


class Env:
    pass


ATTN_DBG = {"eb": True, "deint": True, "vtr": True, "blocks": True, "tail": True}


def phase_consts(S, nc, env):
    k = env.pstk
    nseq, depth = env.nseq, env.depth

    def ld(name, shape, dt, src):
        t = Tile(k, nc, shape, dt)
        S.dma("sp", t.t[:], src, writes=[t.r])
        setattr(env, name, t)
    ld("ident_f", [128, 128], F32, env.i_ident_f)
    ld("ident_b", [128, 128], BF16, env.i_ident_b)
    ld("n1g", [128, depth, 16], F32, env.i_n1g)
    ld("n2g", [128, depth, 16], F32, env.i_n2g)
    ld("fing", [128, 16], F32, env.i_fing)
    ld("bgate", [128, depth, 32], F32, env.i_bgate)
    ld("hcw", [128, depth, 24, 3], F32, env.i_hcw)
    ld("hcb", [128, depth, 24], F32, env.i_hcb)
    ld("hbT", [128, depth, 2, 8], F32, env.i_hbT)
    ld("fcw", [128, depth, 44, 3], F32, env.i_fcw)
    ld("fcb", [128, depth, 44], F32, env.i_fcb)
    ld("adab", [128, depth, 96], F32, env.i_adab)
    env.modT = Tile(k, nc, [128, depth, 96, nseq], F32)
    env.rn = Tile(k, nc, [128, depth, 16], F32)
    env.ones_b = Tile(k, nc, [128, 128], BF16)
    env.ones_f = Tile(k, nc, [128, 128], F32)
    env.ov = Tile(k, nc, [128, 6, 128], BF16)
    env.negpi = Tile(k, nc, [128, 1], F32)
    env.epsb = Tile(k, nc, [128, 1], F32)
    S.op("dve", f_memset(env.ones_b.t[:], 1.0), writes=[env.ones_b.r])
    S.op("dve", f_memset(env.ones_f.t[:], 1.0), writes=[env.ones_f.r])
    S.op("dve", f_memset(env.negpi.t[:], -math.pi * (1 - 2e-6)), writes=[env.negpi.r])
    S.op("dve", f_memset(env.epsb.t[:], EPS), writes=[env.epsb.r])
    S.op("pool", f_memset(env.ov.t[:], 0.0), writes=[env.ov.r])
    for v in range(3):
        for h in range(2):
            lo, hi = (0, 128) if v == 0 else ((64, 128) if v == 1 else (0, 64))
            S.op("pool", f_memset(env.ov.t[lo:hi, 2 * v + h, h * 64:(h + 1) * 64], 1.0), writes=[env.ov.r])
    S.barrier()


def phase_xT(S, nc, env, s):
    with contextlib.ExitStack() as k:
        xi = [Tile(k, nc, [128, D], F32) for _ in range(2)]
        xo = [Tile(k, nc, [128, 16, 512], F32) for _ in range(2)]
        ps = [Tile(k, nc, [128, 512], F32, psum=True) for _ in range(4)]
        ident = env.ident_f
        xTv = env.xT[s].rearrange("(dc p) t -> p dc t", p=128)
        n = 0
        for tt in range(L // 512):
            o = xo[tt % 2]
            for q in range(4):
                tc = tt * 4 + q
                x = xi[tc % 2]
                S.dma("sp", x.t[:], env.xin[s, tc * 128:(tc + 1) * 128, :], writes=[x.r])
                for g4 in range(4):
                    p = ps[n % 4]
                    n += 1
                    items = [(p.t[:, j * 128:(j + 1) * 128], x.t[:, (g4 * 4 + j) * 128:(g4 * 4 + j + 1) * 128])
                             for j in range(4)]
                    S.op("pe", f_trs(items, ident.t[:]), reads=[x.r, ident.r], writes=[p.r])
                    eng = "act" if g4 % 2 else "dve"
                    S.op(eng, f_copy(eng, o.t[:, g4 * 4:(g4 + 1) * 4, q * 128:(q + 1) * 128],
                                     p.t[:].rearrange("p (j t) -> p j t", j=4)),
                         reads=[p.r], pwrites=[o.r])
            S.dma("sp", xTv[:, :, tt * 512:(tt + 1) * 512], o.t[:], reads=[o.r])
        S.barrier()


def phase_mod(S, nc, env):
    nseq, depth = env.nseq, env.depth
    with contextlib.ExitStack() as k:
        cs = Tile(k, nc, [128, 16, nseq], F32)
        wt = [Tile(k, nc, [128, 16, 512], F32) for _ in range(3)]
        ps = Tile(k, nc, [128, 512], F32, psum=True)
        S.dma("sp", cs.t[:], env.i_cT, writes=[cs.r])
        S.op("act", f_act(cs.t[:], cs.t[:], AF.Silu), reads=[cs.r], writes=[cs.r])
        n = 0
        for l in range(depth):
            wv = env.i_ada_w[l].rearrange("(kc p) n -> p kc n", p=128)
            for ct in range(24):
                w = wt[n % 3]
                n += 1
                S.dma("sp", w.t[:], wv[:, :, ct * 512:(ct + 1) * 512], writes=[w.r])
                items = []
                for j in range(4):
                    jg = ct * 4 + j
                    for kc in range(16):
                        items.append((ps.t[:, jg * nseq:(jg + 1) * nseq], w.t[:, kc, j * 128:(j + 1) * 128],
                                      cs.t[:, kc, :], kc == 0, kc == 15))
                S.op("pe", f_mms(items), reads=[w.r, cs.r], writes=[ps.r])
            m = env.modT
            psv = ps.t[:, 0:96 * nseq].rearrange("p (j s) -> p j s", s=nseq)
            for s in range(nseq):
                S.op("dve", f_tt(m.t[:, l, :, s], psv[:, :, s], env.adab.t[:, l, :], ALU.add),
                     reads=[ps.r, env.adab.r], writes=[m.r])
                S.op("dve", f_stt(m.t[:, l, 16:32, s], m.t[:, l, 16:32, s], 1.0, env.n1g.t[:, l, :], ALU.add, ALU.mult),
                     reads=[m.r, env.n1g.r], writes=[m.r])
                S.op("dve", f_stt(m.t[:, l, 64:80, s], m.t[:, l, 64:80, s], 1.0, env.n2g.t[:, l, :], ALU.add, ALU.mult),
                     reads=[m.r, env.n2g.r], writes=[m.r])
        S.barrier()


def phase_eb1(S, nc, env):
    with contextlib.ExitStack() as k:
        rb = Tile(k, nc, [32, 24], F32)
        sel = Tile(k, nc, [32, 3, WLEN], F32)
        lt = [Tile(k, nc, [32, 128], F32) for _ in range(2)]
        wsb = [Tile(k, nc, [128, WLEN], F32) for _ in range(2)]
        ps = [Tile(k, nc, [128, 512], F32, psum=True) for _ in range(2)]
        S.dma("sp", rb.t[:], env.i_rel_bias, writes=[rb.r])
        S.dma("sp", sel.t[:], env.i_selw, writes=[sel.r])
        S.op("act", f_act(rb.t[:], rb.t[:], AF.Exp), reads=[rb.r], writes=[rb.r])
        for gh in range(24):
            g = gh // 8
            a = lt[gh % 2]
            p = ps[gh % 2]
            w = wsb[gh % 2]
            S.op("dve", f_ts(a.t[:], env.ones_f.t[0:32, :], rb.t[:, gh:gh + 1], None, ALU.mult),
                 reads=[env.ones_f.r, rb.r], writes=[a.r])
            S.op("pe", f_mms([(p.t[:, 0:WLEN], a.t[:], sel.t[:, g, :], True, True)]), reads=[a.r, sel.r], writes=[p.r])
            S.op("act", f_copy("act", w.t[:], p.t[:, 0:WLEN]), reads=[p.r], writes=[w.r])
            S.dma("sp", env.Wd[gh], w.t[:], reads=[w.r])
        S.barrier()


def load_eb(S, nc, env, k):
    ebf = [Tile(k, nc, [128, 2, 128], F32) for _ in range(2)]
    eb = Tile(k, nc, [128, 24, 2, 128], BF16)
    wh = env.Wd_h
    for gh in range(24):
        st_ = ebf[gh % 2]
        for c in range(2):
            off = gh * 128 * WLEN + (255 if c == 0 else 127)
            src = bass.AP(tensor=wh, offset=off, ap=[[WLEN - 1, 128], [1, 128]])
            S.dma("sp", st_.t[:, c, :], src, pwrites=[st_.r])
        S.op("dve", f_copy("dve", eb.t[:, gh, :, :], st_.t[:]), reads=[st_.r], pwrites=[eb.r])
    return eb


def sin_layer(S, nc, env, k, ps, dst, dres, bcol, fcol, tmp, tmp2, fpr):
    MAGIC = 12582912.0
    S.op("dve", f_ts(tmp.t[0:64, :], ps.t[0:64, :], bcol, fcol, ALU.add, ALU.mult), reads=[ps.r, fpr], writes=[tmp.r])
    S.op("dve", f_ts(tmp2.t[0:64, :], tmp.t[0:64, :], 1.0 / (2 * math.pi), MAGIC, ALU.mult, ALU.add),
         reads=[tmp.r], writes=[tmp2.r])
    S.op("dve", f_ts(tmp2.t[0:64, :], tmp2.t[0:64, :], MAGIC, -2 * math.pi, ALU.subtract, ALU.mult),
         reads=[tmp2.r], writes=[tmp2.r])
    S.op("dve", f_tt(tmp.t[0:64, :], tmp.t[0:64, :], tmp2.t[0:64, :], ALU.add), reads=[tmp.r, tmp2.r], writes=[tmp.r])
    S.op("act", f_act(dst, tmp.t[0:64, :], AF.Sin, scale=(1 - 2e-6)), reads=[tmp.r], pwrites=[dres])


def phase_filter(S, nc, env, l):
    with contextlib.ExitStack() as k:
        featT = Tile(k, nc, [33, L], F32)
        w1 = Tile(k, nc, [33, 64], F32)
        w2 = Tile(k, nc, [64, 64], F32)
        w3 = Tile(k, nc, [64, 4096], F32)
        fp = Tile(k, nc, [64, 4], F32)
        h1 = Tile(k, nc, [64, L], F32)
        h2 = Tile(k, nc, [64, L], F32)
        tmp = [Tile(k, nc, [128, 512], F32) for _ in range(2)]
        dec = [Tile(k, nc, [128, 512], F32) for _ in range(2)]
        hf = [Tile(k, nc, [128, 512], F32) for _ in range(2)]
        hb = [Tile(k, nc, [128, 512], F32) for _ in range(2)]
        acc = Tile(k, nc, [128, 512], F32)
        ee = Tile(k, nc, [128, 32, 512], BF16)
        dd = Tile(k, nc, [128, 32, 512], BF16)
        Ft = [Tile(k, nc, [128, 32, 128], BF16) for _ in range(3)]
        ko = [Tile(k, nc, [128, 512], F32) for _ in range(2)]
        alt = Tile(k, nc, [128, 2], BF16)
        nyq = Tile(k, nc, [1, 512], F32)
        ps = [Tile(k, nc, [128, 512], F32, psum=True) for _ in range(6)]
        psn = Tile(k, nc, [128, 512], F32, psum=True)
        S.dma("sp", featT.t[:], env.i_featT, writes=[featT.r])
        S.dma("sp", w1.t[:], env.i_fw1[l], writes=[w1.r])
        S.dma("sp", w2.t[:], env.i_fw2[l], writes=[w2.r])
        S.dma("sp", w3.t[:], env.i_fw3[l], writes=[w3.r])
        S.dma("sp", fp.t[:], env.i_fpar[l], writes=[fp.r])
        S.dma("sp", alt.t[:], env.i_alt, writes=[alt.r])
        for tt in range(8):
            p = ps[tt % 2]
            S.op("pe", f_mms([(p.t[0:64, :], w1.t[:], featT.t[:, tt * 512:(tt + 1) * 512], True, True)]),
                 reads=[w1.r, featT.r], writes=[p.r])
            sin_layer(S, nc, env, k, p, h1.t[:, tt * 512:(tt + 1) * 512], h1.r, fp.t[:, 0:1], fp.t[:, 1:2], tmp[tt % 2], hf[tt % 2], fp.r)
        for tt in range(8):
            p = ps[tt % 2]
            S.op("pe", f_mms([(p.t[0:64, :], w2.t[:], h1.t[:, tt * 512:(tt + 1) * 512], True, True)]),
                 reads=[w2.r, h1.r], writes=[p.r])
            sin_layer(S, nc, env, k, p, h2.t[:, tt * 512:(tt + 1) * 512], h2.r, fp.t[:, 2:3], fp.t[:, 3:4], tmp[tt % 2], hf[tt % 2], fp.r)
        if env.dbg and l == 0:
            S.dma("sp", env.dbg_h1, h1.t[:], reads=[h1.r])
            S.dma("sp", env.dbg_h2, h2.t[:], reads=[h2.r])
        n = 0
        nf = 0
        nk = 0
        for ct in range(4):
            o, ch = ct // 2, ct % 2
            cf = o * 1024 + ch * 512
            cb = 2048 + cf
            for lc in range(32):
                pf, pb = ps[(2 * n) % 4], ps[(2 * n + 1) % 4]
                d_, f_, b_ = dec[n % 2], hf[n % 2], hb[n % 2]
                n += 1
                S.dma("sp", d_.t[:], env.i_decay[lc * 128:(lc + 1) * 128, ch * 512:(ch + 1) * 512], writes=[d_.r])
                S.op("pe", f_mms([(pf.t[:], h2.t[:, lc * 128:(lc + 1) * 128], w3.t[:, cf:cf + 512], True, True)]),
                     reads=[h2.r, w3.r], writes=[pf.r])
                S.op("pe", f_mms([(pb.t[:], h2.t[:, lc * 128:(lc + 1) * 128], w3.t[:, cb:cb + 512], True, True)]),
                     reads=[h2.r, w3.r], writes=[pb.r])
                S.op("dve", f_tt(f_.t[:], pf.t[:], d_.t[:], ALU.mult), reads=[pf.r, d_.r], writes=[f_.r])
                S.op("dve", f_tt(b_.t[:], pb.t[:], d_.t[:], ALU.mult), reads=[pb.r, d_.r], writes=[b_.r])
                if lc == 0:
                    S.op("dve", f_memset(b_.t[0:1, :], 0.0), writes=[b_.r])
                    S.op("pool", f_memset(acc.t[:], 0.0), writes=[acc.r])
                ta, tb = tmp[0], tmp[1]
                S.op("act", f_act(ta.t[:], f_.t[:], AF.Abs), reads=[f_.r], writes=[ta.r])
                S.op("act", f_act(tb.t[:], b_.t[:], AF.Abs), reads=[b_.r], writes=[tb.r])
                S.op("pool", f_tt(acc.t[:], acc.t[:], ta.t[:], ALU.add), reads=[ta.r, acc.r], writes=[acc.r])
                S.op("pool", f_tt(acc.t[:], acc.t[:], tb.t[:], ALU.add), reads=[tb.r, acc.r], writes=[acc.r])
                S.op("dve", f_tt(ee.t[:, lc, :], f_.t[:], b_.t[:], ALU.add), reads=[f_.r, b_.r], pwrites=[ee.r])
                S.op("pool", f_tt(dd.t[:, lc, :], f_.t[:], b_.t[:], ALU.subtract), reads=[f_.r, b_.r], pwrites=[dd.r])
            if env.dbg and l == 0 and ct == 0:
                S.dma("sp", env.dbg_e, ee.t[:], reads=[ee.r])
            items = [(psn.t[:, j:j + 1], acc.t[:, j * 128:(j + 1) * 128], env.ones_f.t[:, 0:1], True, True) for j in range(4)]
            S.op("pe", f_mms(items), reads=[acc.r, env.ones_f.r], writes=[psn.r])
            c0 = o * 8 + ch * 4
            S.op("dve", (lambda e, c0=c0: e.reciprocal(out=env.rn.t[:, l, c0:c0 + 4], in_=psn.t[:, 0:4])),
                 reads=[psn.r], writes=[env.rn.r])
            S.op("pe", f_mms([(psn.t[0:1, :], alt.t[:, 0:1], ee.t[:, lc, :], lc == 0, lc == 31) for lc in range(32)]),
                 reads=[alt.r, ee.r], writes=[psn.r])
            S.op("act", f_copy("act", nyq.t[:], psn.t[0:1, :]), reads=[psn.r], writes=[nyq.r])
            def load_f(fc_):
                ft_ = Ft[fc_ % 3]
                S.dma("sp", ft_.t[:], env.i_Fh[fc_].rearrange("p (t f) -> p t f", f=128), writes=[ft_.r])
            load_f(0)
            load_f(1)
            for fc in range(64):
                ft = Ft[fc % 3]
                src = ee if fc < 32 else dd
                p = ps[4 + (nk % 2)]
                kk = ko[nk % 2]
                nk += 1
                S.op("pe", f_mms([(p.t[:], ft.t[:, tc, :], src.t[:, tc, :], tc == 0, tc == 31) for tc in range(32)]),
                     reads=[ft.r, src.r], writes=[p.r])
                if fc + 2 < 64:
                    load_f(fc + 2)
                eng = "act" if fc % 2 else "dve"
                S.op(eng, f_copy(eng, kk.t[:], p.t[:]), reads=[p.r], writes=[kk.r])
                if fc == 32:
                    S.op("dve", f_copy("dve", kk.t[0:1, :], nyq.t[:]), reads=[nyq.r], writes=[kk.r])
                S.dma("sp", env.Kd[fc // 32, (fc % 32) * 128:(fc % 32 + 1) * 128, ct * 512:(ct + 1) * 512], kk.t[:],
                      reads=[kk.r])
        S.barrier()


def norm_subtile(S, nc, env, k, bufs, xTs, t0, ntok, A, B, hT_ap, hT_res, out_f32=None):
    x, sq, pss, rstd, tmp = bufs
    xv = xTs.rearrange("(dc p) t -> p dc t", p=128)
    S.dma("sp", x.t[:, :, 0:ntok], xv[:, :, t0:t0 + ntok], writes=[x.r])
    S.op("act", f_act(sq.t[:, :, 0:ntok], x.t[:, :, 0:ntok], AF.Square), reads=[x.r], writes=[sq.r])
    S.op("pe", f_mms([(pss.t[:, 0:ntok], env.ones_b.t[:], sq.t[:, dc, 0:ntok], dc == 0, dc == 15) for dc in range(16)]),
         reads=[sq.r, env.ones_b.r], writes=[pss.r])
    S.op("act", f_act(rstd.t[:, 0:ntok], pss.t[:, 0:ntok], AF.Sqrt, bias=env.epsb.t[:], scale=1.0 / D),
         reads=[pss.r, env.epsb.r], writes=[rstd.r])
    S.op("dve", (lambda e: e.reciprocal(out=rstd.t[:, 0:ntok], in_=rstd.t[:, 0:ntok])), reads=[rstd.r], writes=[rstd.r])
    for dc in range(16):
        t = tmp[dc % 2]
        if out_f32 is None:
            S.op("dve", f_stt(t.t[:, 0:ntok], x.t[:, dc, 0:ntok], A[:, dc:dc + 1], rstd.t[:, 0:ntok], ALU.mult, ALU.mult),
                 reads=[x.r, rstd.r, env.modT.r], writes=[t.r])
            S.op("act", f_act(hT_ap(dc), t.t[:, 0:ntok], AF.Identity, bias=B[:, dc:dc + 1], scale=1.0),
                 reads=[t.r, env.modT.r], pwrites=[hT_res])
        else:
            S.op("dve", f_stt(out_f32(dc), x.t[:, dc, 0:ntok], A[:, dc:dc + 1], rstd.t[:, 0:ntok], ALU.mult, ALU.mult),
                 reads=[x.r, rstd.r], pwrites=[hT_res])


def phase_proj(S, nc, env, s, l, which):
    TT = 1024
    SUB = 256
    m = env.modT
    if which == "in":
        W = env.i_w_in[l]
        nch = INC // 128
        A, B = m.t[:, l, 16:32, s], m.t[:, l, 0:16, s]
    else:
        W = env.i_ffn_up[l]
        nch = 2 * DFF // 128
        A, B = m.t[:, l, 64:80, s], m.t[:, l, 48:64, s]
    Wv = W.rearrange("(kc p) n -> p kc n", p=128)
    with contextlib.ExitStack() as k:
        hT = [Tile(k, nc, [128, 16, TT], BF16) for _ in range(2)]
        x = Tile(k, nc, [128, 16, SUB], F32)
        sq = Tile(k, nc, [128, 16, SUB], BF16)
        rstd = Tile(k, nc, [128, SUB], F32)
        tmp = [Tile(k, nc, [128, SUB], F32) for _ in range(2)]
        pss = Tile(k, nc, [128, 512], F32, psum=True)
        wt = [Tile(k, nc, [128, 16, 512], BF16) for _ in range(3)]
        ob = [Tile(k, nc, [128, TT], BF16) for _ in range(3)]
        ps = [[Tile(k, nc, [128, 512], F32, psum=True) for _ in range(2)] for _ in range(3)]
        bufs = (x, sq, pss, rstd, tmp)
        ntile = L // TT

        def do_norm(tt):
            h = hT[tt % 2]
            for sb in range(TT // SUB):
                t0 = tt * TT + sb * SUB
                norm_subtile(S, nc, env, k, bufs, env.xT[s], t0, SUB, A, B,
                             (lambda dc, h=h, sb=sb: h.t[:, dc, sb * SUB:(sb + 1) * SUB]), h.r)
        do_norm(0)
        nw = 0
        no = 0
        for tt in range(ntile):
            h = hT[tt % 2]
            for j in range(nch):
                if j % 4 == 0:
                    w = wt[nw % 3]
                    nw += 1
                    S.dma("pool", w.t[:], Wv[:, :, j * 128:(j + 4) * 128], writes=[w.r])
                pp = ps[no % 3]
                o = ob[no % 3]
                no += 1
                jj = j % 4
                for hf in range(2):
                    S.op("pe", f_mms([(pp[hf].t[:], w.t[:, kc, jj * 128:(jj + 1) * 128], h.t[:, kc, hf * 512:(hf + 1) * 512],
                                       kc == 0, kc == 15) for kc in range(16)]),
                         reads=[w.r, h.r], writes=[pp[hf].r])
                for hf in range(2):
                    oap = o.t[:, hf * 512:(hf + 1) * 512]
                    if which == "in" and j >= 60:
                        S.op("act", f_act(oap, pp[hf].t[:], AF.Sigmoid, bias=env.bgate.t[:, l, j - 60:j - 59], scale=1.0),
                             reads=[pp[hf].r, env.bgate.r], pwrites=[o.r])
                    else:
                        eng = "dve" if (hf + j) % 2 == 0 else "act"
                        S.op(eng, f_copy(eng, oap, pp[hf].t[:]), reads=[pp[hf].r], pwrites=[o.r])
                if which == "in":
                    dst = env.uT[s][j * 128:(j + 1) * 128, tt * TT:(tt + 1) * TT]
                elif j < 44:
                    dst = env.aT[s][j * 128:(j + 1) * 128, tt * TT:(tt + 1) * TT]
                else:
                    dst = env.gT[s][(j - 44) * 128:(j - 43) * 128, tt * TT:(tt + 1) * TT]
                S.dma("sp", dst, o.t[:], reads=[o.r])
                if j == nch // 2 and tt + 1 < ntile:
                    do_norm(tt + 1)
        S.barrier()


def phase_dwconv(S, nc, env, s, l):
    with contextlib.ExitStack() as k:
        u = [Tile(k, nc, [128, L], BF16) for _ in range(2)]
        o = [Tile(k, nc, [128, L], F32) for _ in range(2)]
        ob = [Tile(k, nc, [128, L], BF16) for _ in range(2)]
        w = env.hcw.t

        def load(rc_):
            S.dma("sp", u[rc_ % 2].t[:], env.uT[s][rc_ * 128:(rc_ + 1) * 128, :], writes=[u[rc_ % 2].r])
        load(0)
        for rc in range(24):
            uu, oo, bb = u[rc % 2], o[rc % 2], ob[rc % 2]
            S.op("act", f_act(oo.t[:], uu.t[:], AF.Identity, bias=env.hcb.t[:, l, rc:rc + 1], scale=w[:, l, rc, 1:2]),
                 reads=[uu.r, env.hcw.r, env.hcb.r], writes=[oo.r])
            if rc + 1 < 24:
                load(rc + 1)
            S.op("dve", f_stt(oo.t[:, 1:L], uu.t[:, 0:L - 1], w[:, l, rc, 0:1], oo.t[:, 1:L], ALU.mult, ALU.add),
                 reads=[uu.r, oo.r], writes=[oo.r])
            S.op("dve", f_stt(bb.t[:, 0:L - 1], uu.t[:, 1:L], w[:, l, rc, 2:3], oo.t[:, 0:L - 1], ALU.mult, ALU.add),
                 reads=[uu.r, oo.r], writes=[bb.r])
            S.op("pool", f_copy("pool", bb.t[:, L - 1:L], oo.t[:, L - 1:L]), reads=[oo.r, bb.r], writes=[bb.r])
            S.dma("sp", env.ucT[s][rc * 128:(rc + 1) * 128, :], bb.t[:], reads=[bb.r])
        S.barrier()


def phase_hyena(S, nc, env, s, l, hh, o):
    c0 = hh * 512
    zsrc = env.ucT[s][0:1024] if o == 0 else env.zmid[s]
    zdst = env.zmid[s] if o == 0 else env.zfT[s]
    xo = env.ucT[s][1024 * (o + 1):1024 * (o + 2)]
    with contextlib.ExitStack() as k:
        zcs = [Tile(k, nc, [128, L], BF16) for _ in range(2)]
        zT = Tile(k, nc, [128, 32, 512], BF16)
        Y = Tile(k, nc, [128, 64, 512], BF16)
        Y0 = Res()
        Ft = [Tile(k, nc, [128, 32, 128], BF16) for _ in range(3)]
        Kt = [Tile(k, nc, [128, 2, 512], F32) for _ in range(2)]
        Xs = [Tile(k, nc, [128, 2, 512], F32) for _ in range(2)]
        tmp = [Tile(k, nc, [128, 512], F32) for _ in range(4)]
        Gt = [Tile(k, nc, [128, 4, 512], BF16) for _ in range(2)]
        zt = [Tile(k, nc, [128, 512], BF16) for _ in range(2)]
        xt = [Tile(k, nc, [128, 512], BF16) for _ in range(2)]
        zb = [Tile(k, nc, [128, 512], F32) for _ in range(2)]
        a32 = [Tile(k, nc, [128, 512], F32) for _ in range(2)]
        zo = [Tile(k, nc, [128, 512], BF16) for _ in range(2)]
        pst = [Tile(k, nc, [128, 512], BF16, psum=True) for _ in range(2)]
        pX = [Tile(k, nc, [128, 512], F32, psum=True) for _ in range(4)]
        n = 0
        for cc in range(4):
            zc = zcs[cc % 2]
            S.dma("sp", zc.t[:], zsrc[c0 + cc * 128:c0 + (cc + 1) * 128, :], writes=[zc.r])
            for tg in range(8):
                p = pst[n % 2]
                n += 1
                S.op("pe", f_trs([(p.t[:, j * 128:(j + 1) * 128], zc.t[:, (tg * 4 + j) * 128:(tg * 4 + j + 1) * 128])
                                  for j in range(4)], env.ident_b.t[:]),
                     reads=[zc.r, env.ident_b.r], writes=[p.r])
                eng = "act" if tg % 2 else "dve"
                S.op(eng, f_copy(eng, zT.t[:, tg * 4:(tg + 1) * 4, cc * 128:(cc + 1) * 128],
                                 p.t[:, 0:512].rearrange("p (j c) -> p j c", j=4)),
                     reads=[p.r], pwrites=[zT.r])
        nf = 0
        for fp in range(32):
            kt = Kt[fp % 2]
            xs = Xs[fp % 2]
            fts = []
            for part in range(2):
                ft = Ft[nf % 3]
                nf += 1
                fc = part * 32 + fp
                S.dma("sp", ft.t[:], env.i_Fh[fc].rearrange("p (t f) -> p t f", f=128), writes=[ft.r])
                fts.append(ft)
            S.dma("sp", kt.t[:, 0, :], env.Kd[0, fp * 128:(fp + 1) * 128, o * 1024 + c0:o * 1024 + c0 + 512], pwrites=[kt.r])
            S.dma("sp", kt.t[:, 1, :], env.Kd[1, fp * 128:(fp + 1) * 128, o * 1024 + c0:o * 1024 + c0 + 512], pwrites=[kt.r])
            for part in range(2):
                ft = fts[part]
                p = pX[(fp % 2) * 2 + part]
                S.op("pe", f_mms([(p.t[:], ft.t[:, tc, :], zT.t[:, tc, :], tc == 0, tc == 31) for tc in range(32)]),
                     reads=[ft.r, zT.r], writes=[p.r])
                S.op("act", f_copy("act", xs.t[:, part, :], p.t[:]), reads=[p.r], pwrites=[xs.r])
            yres = [Y0] if fp == 0 else []
            S.op("dve", f_tt(tmp[0].t[:], xs.t[:, 0, :], kt.t[:, 0, :], ALU.mult), reads=[xs.r, kt.r], writes=[tmp[0].r])
            S.op("pool", f_tt(tmp[1].t[:], xs.t[:, 1, :], kt.t[:, 1, :], ALU.mult), reads=[xs.r, kt.r], writes=[tmp[1].r])
            S.op("dve", f_tt(Y.t[:, fp, :], tmp[0].t[:], tmp[1].t[:], ALU.subtract), reads=[tmp[0].r, tmp[1].r],
                 pwrites=[Y.r], writes=yres)
            S.op("pool", f_tt(tmp[2].t[:], xs.t[:, 0, :], kt.t[:, 1, :], ALU.mult), reads=[xs.r, kt.r], writes=[tmp[2].r])
            S.op("dve", f_tt(tmp[3].t[:], xs.t[:, 1, :], kt.t[:, 0, :], ALU.mult), reads=[xs.r, kt.r], writes=[tmp[3].r])
            S.op("pool", f_tt(Y.t[:, 32 + fp, :], tmp[2].t[:], tmp[3].t[:], ALU.add), reads=[tmp[2].r, tmp[3].r],
                 pwrites=[Y.r], writes=yres)
            if fp == 0:
                S.op("dve", f_tt(Y.t[0:1, 0, :], xs.t[0:1, 0, :], kt.t[0:1, 0, :], ALU.mult), reads=[xs.r, kt.r], writes=[Y0])
                S.op("dve", f_tt(Y.t[0:1, 32, :], xs.t[0:1, 1, :], kt.t[0:1, 1, :], ALU.mult), reads=[xs.r, kt.r], writes=[Y0])
        ne = 0

        def load_g(idx_):
            gt_ = Gt[idx_ % 2]
            S.dma("sp", gt_.t[:], env.i_Gh[idx_ // 16, idx_ % 16].rearrange("p (j t) -> p j t", t=512), writes=[gt_.r])
        load_g(0)
        load_g(1)
        for tt in range(8):
            for g4 in range(16):
                idx = tt * 16 + g4
                gt = Gt[idx % 2]
                items = []
                for j in range(4):
                    fk = g4 * 4 + j
                    for cc in range(4):
                        items.append((pX[cc].t[:], Y.t[:, fk, cc * 128:(cc + 1) * 128], gt.t[:, j, :], fk == 0, fk == 63))
                S.op("pe", f_mms(items), reads=[Y.r, Y0, gt.r], writes=[pX[c].r for c in range(4)])
                if idx + 2 < 128:
                    load_g(idx + 2)
            for cc in range(4):
                i = ne % 2
                ne += 1
                ci = hh * 4 + cc
                rows = slice(c0 + cc * 128, c0 + (cc + 1) * 128)
                cols = slice(tt * 512, (tt + 1) * 512)
                S.dma("sp", zt[i].t[:], zsrc[rows, cols], writes=[zt[i].r])
                S.dma("sp", xt[i].t[:], xo[rows, cols], writes=[xt[i].r])
                S.op("act", f_act(zb[i].t[:], zt[i].t[:], AF.Copy, scale=env.hbT.t[:, l, o, ci:ci + 1]),
                     reads=[zt[i].r, env.hbT.r], writes=[zb[i].r])
                S.op("dve", f_stt(a32[i].t[:], pX[cc].t[:], env.rn.t[:, l, o * 8 + ci:o * 8 + ci + 1], zb[i].t[:],
                                  ALU.mult, ALU.add),
                     reads=[pX[cc].r, env.rn.r, zb[i].r], writes=[a32[i].r])
                S.op("pool", f_tt(zo[i].t[:], a32[i].t[:], xt[i].t[:], ALU.mult), reads=[a32[i].r, xt[i].r], writes=[zo[i].r])
                S.dma("sp", zdst[rows, cols], zo[i].t[:], reads=[zo[i].r])
        S.barrier()


def phase_attn(S, nc, env, s, l):
    QOFF = 3 * HW
    with contextlib.ExitStack() as k:
        if ATTN_DBG["eb"]:
            eb = load_eb(S, nc, env, k)
        else:
            eb = Tile(k, nc, [128, 24, 2, 128], BF16)
            S.op("pool", f_memset(eb.t[:], 1.0), writes=[eb.r])
        raw = [Tile(k, nc, [128, L], BF16) for _ in range(3)]
        qd = Tile(k, nc, [128, L], BF16)
        kd1 = Tile(k, nc, [128, L + 128 * 16], BF16)
        vd1 = Tile(k, nc, [128, L + 128 * 16], BF16)
        kd = [kd1] * 3
        vd = [vd1] * 3
        Vt0 = Tile(k, nc, [128, 68, 128], BF16)
        Vt1 = Tile(k, nc, [128, 68, 128], BF16)
        Vt = [[Vt0, Vt1]] * 3
        Num = Tile(k, nc, [128, L], F32)
        Den = Tile(k, nc, [128, L], F32)
        yb = Tile(k, nc, [128, L], BF16)
        p32 = [Tile(k, nc, [128, 512], F32) for _ in range(4)]
        pbf = [Tile(k, nc, [128, 512], BF16) for _ in range(4)]
        pst = [Tile(k, nc, [128, 512], BF16, psum=True) for _ in range(2)]
        pS = [Tile(k, nc, [128, 512], F32, psum=True) for _ in range(4)]
        pO = [Tile(k, nc, [128, 512], F32, psum=True) for _ in range(2)]
        S.op("pool", f_memset(Vt0.t[:], 0.0), writes=[Vt0.r])
        S.op("pool", f_memset(Vt1.t[:], 0.0), writes=[Vt1.r])
        nt = 0
        nb = 0
        npo = 0
        for hp in range(4):
            for g, (_, d) in enumerate(GROUPS):
                M = L // d
                MP = M + 128
                r0 = QOFF + g * 512 + hp * 128
                for i3, off in enumerate((0, AW, 2 * AW)):
                    S.dma("sp", raw[i3].t[:], env.uT[s][r0 + off:r0 + off + 128, :], writes=[raw[i3].r])
                qv = qd.t[:].rearrange("p (d m) -> p d m", d=d)
                kv = kd[g].t[:, 0:d * MP].rearrange("p (d m) -> p d m", d=d)
                vv = vd[g].t[:, 0:d * MP].rearrange("p (d m) -> p d m", d=d)
                S.op("pool", f_memset(kd[g].t[:, 0:d * MP], 0.0), writes=[kd[g].r])
                S.op("pool", f_memset(vd[g].t[:, 0:d * MP], 0.0), writes=[vd[g].r])
                if ATTN_DBG["deint"]:
                    S.op("dve", f_copy("dve", qv, raw[0].t[:].rearrange("p (m d) -> p d m", d=d)), reads=[raw[0].r], writes=[qd.r])
                    S.op("pool", f_copy("pool", kv[:, :, 64:64 + M], raw[1].t[:].rearrange("p (m d) -> p d m", d=d)),
                         reads=[raw[1].r], pwrites=[kd[g].r])
                    S.op("act", f_copy("act", vv[:, :, 64:64 + M], raw[2].t[:].rearrange("p (m d) -> p d m", d=d)),
                         reads=[raw[2].r], pwrites=[vd[g].r])
                nchunk = MP // 128
                tot = d * nchunk
                for t0 in range(0, tot if ATTN_DBG["vtr"] else 0, 4):
                    p = pst[nt % 2]
                    nt += 1
                    items = []
                    cnt = min(4, tot - t0)
                    for j in range(cnt):
                        r, i = divmod(t0 + j, nchunk)
                        items.append((p.t[:, j * 128:(j + 1) * 128], vv[:, r, i * 128:(i + 1) * 128]))
                    S.op("pe", f_trs(items, env.ident_b.t[:]), reads=[vd[g].r, env.ident_b.r], writes=[p.r])
                    pv = p.t[:, 0:512].rearrange("p (j c) -> p j c", j=4)
                    S.op("act", f_copy("act", Vt[g][0].t[:, t0:t0 + cnt, 0:64], pv[:, 0:cnt, 0:64]), reads=[p.r], pwrites=[Vt[g][0].r])
                    S.op("dve", f_copy("dve", Vt[g][1].t[:, t0:t0 + cnt, 64:128], pv[:, 0:cnt, 64:128]), reads=[p.r], pwrites=[Vt[g][1].r])
                nblk = M // 128

                def stage1(r, b2):
                    nonlocal nb
                    pbs = []
                    for h in range(2):
                        gh = g * 8 + hp * 2 + h
                        ps_ = pS[nb % 4]
                        pf_ = p32[nb % 4]
                        pb_ = pbf[nb % 4]
                        nb += 1
                        hs = slice(h * 64, (h + 1) * 64)
                        items = []
                        for qb in range(2):
                            b = b2 + qb
                            for c in range(2):
                                items.append((ps_.t[:, (qb * 2 + c) * 128:(qb * 2 + c + 1) * 128],
                                              kv[hs, r, (b + c) * 128:(b + c + 1) * 128],
                                              qv[hs, r, b * 128:(b + 1) * 128], True, True))
                        S.op("pe", f_mms(items), reads=[kd[g].r, qd.r], writes=[ps_.r])
                        S.op("act", f_act(pf_.t[:], ps_.t[:], AF.Exp, scale=0.125), reads=[ps_.r], writes=[pf_.r])
                        for qb in range(2):
                            S.op("dve", f_tt(pb_.t[:, qb * 256:(qb + 1) * 256], pf_.t[:, qb * 256:(qb + 1) * 256],
                                             eb.t[:, gh, :, :].rearrange("p c q -> p (c q)"), ALU.mult),
                                 reads=[pf_.r, eb.r], pwrites=[pb_.r])
                        pbs.append(pb_)
                    return pbs

                def stage2(r, b2, pbs):
                    nonlocal npo
                    po_ = pO[npo % 2]
                    npo += 1
                    mm = []
                    for qb in range(2):
                        b = b2 + qb
                        for isden in range(2):
                            oreg = po_.t[:, isden * 256 + qb * 128:isden * 256 + (qb + 1) * 128]
                            idx = 0
                            for h2 in range(2):
                                for c in range(2):
                                    vi = r * nchunk + b + c
                                    vsel = 1 if b + c == 0 else (2 if b + c == nchunk - 1 else 0)
                                    rhs = pbs[h2].t[:, (qb * 2 + c) * 128:(qb * 2 + c + 1) * 128]
                                    lhs = env.ov.t[:, 2 * vsel + h2, :] if isden else Vt[g][h2].t[:, vi, :]
                                    mm.append((oreg, lhs, rhs, idx == 0, idx == 3))
                                    idx += 1
                    S.op("pe", f_mms(mm), reads=[pbs[0].r, pbs[1].r, Vt[g][0].r, Vt[g][1].r, env.ov.r], writes=[po_.r])
                    t_lo = r + d * 128 * b2
                    sl = slice(t_lo, t_lo + d * 255 + 1, d) if d > 1 else slice(t_lo, t_lo + 256)
                    if g == 0:
                        S.op("dve", f_copy("dve", Num.t[:, sl], po_.t[:, 0:256]), reads=[po_.r], writes=[Num.r])
                        S.op("act", f_copy("act", Den.t[:, sl], po_.t[:, 256:512]), reads=[po_.r], writes=[Den.r])
                    else:
                        S.op("dve", f_tt(Num.t[:, sl], Num.t[:, sl], po_.t[:, 0:256], ALU.add),
                             reads=[po_.r, Num.r], writes=[Num.r])
                        S.op("dve", f_tt(Den.t[:, sl], Den.t[:, sl], po_.t[:, 256:512], ALU.add),
                             reads=[po_.r, Den.r], writes=[Den.r])

                work = [(r, b2) for r in range(d if ATTN_DBG["blocks"] else 0) for b2 in range(0, nblk, 2)]
                prev = None
                for (r, b2) in work:
                    cur = stage1(r, b2)
                    if prev is not None:
                        stage2(*prev)
                    prev = (r, b2, cur)
                if prev is not None:
                    stage2(*prev)
            if ATTN_DBG["tail"]:
                S.op("dve", (lambda e: e.reciprocal(out=Den.t[:], in_=Den.t[:])), reads=[Den.r], writes=[Den.r])
                S.op("pool", f_tt(yb.t[:], Num.t[:], Den.t[:], ALU.mult), reads=[Num.r, Den.r], writes=[yb.r])
                S.dma("sp", env.yatT[s][hp * 128:(hp + 1) * 128, :], yb.t[:], reads=[yb.r])
        S.barrier()


def residual_tile(S, nc, env, s, pp, dchunk, t0, TT, gcol, xr, xn):
    rows = slice(dchunk * 128, (dchunk + 1) * 128)
    S.dma("sp", xr.t[:], env.xT[s][rows, t0:t0 + TT], writes=[xr.r])
    for hf in range(TT // 512):
        S.op("dve", f_stt(xn.t[:, hf * 512:(hf + 1) * 512], pp[hf].t[:], gcol, xr.t[:, hf * 512:(hf + 1) * 512],
                          ALU.mult, ALU.add),
             reads=[pp[hf].r, xr.r, env.modT.r], pwrites=[xn.r])
    S.dma("sp", env.xT[s][rows, t0:t0 + TT], xn.t[:], reads=[xn.r])


def phase_merge(S, nc, env, s, l):
    TT = 1024
    GOFF = 3 * HW + 3 * AW
    m = env.modT
    with contextlib.ExitStack() as k:
        zf = Tile(k, nc, [128, 8, TT], BF16)
        ya = Tile(k, nc, [128, 4, TT], BF16)
        mg = Tile(k, nc, [128, 16, TT], BF16)
        wh = [Tile(k, nc, [128, 8, 512], BF16) for _ in range(2)]
        wa = [Tile(k, nc, [128, 4, 512], BF16) for _ in range(2)]
        wo = [Tile(k, nc, [128, 16, 512], BF16) for _ in range(2)]
        gh_ = [Tile(k, nc, [128, TT], BF16) for _ in range(2)]
        ga_ = [Tile(k, nc, [128, TT], BF16) for _ in range(2)]
        t1 = [Tile(k, nc, [128, TT], F32) for _ in range(2)]
        t2 = [Tile(k, nc, [128, TT], F32) for _ in range(2)]
        xr = [Tile(k, nc, [128, TT], F32) for _ in range(2)]
        xn = [Tile(k, nc, [128, TT], F32) for _ in range(2)]
        ps = [[Tile(k, nc, [128, 512], F32, psum=True) for _ in range(2)] for _ in range(4)]
        Whv = env.i_w_br_hy[l].rearrange("(kc p) n -> p kc n", p=128)
        Wav = env.i_w_br_attn[l].rearrange("(kc p) n -> p kc n", p=128)
        Wov = env.i_w_out[l].rearrange("(kc p) n -> p kc n", p=128)
        nn = 0
        for tt in range(L // TT):
            cols = slice(tt * TT, (tt + 1) * TT)
            S.dma("sp", zf.t[:], env.zfT[s].rearrange("(c p) t -> p c t", p=128)[:, :, cols], writes=[zf.r])
            S.dma("sp", ya.t[:], env.yatT[s].rearrange("(c p) t -> p c t", p=128)[:, :, cols], writes=[ya.r])
            for dj in range(16):
                i = dj % 2
                if dj % 4 == 0:
                    w_h, w_a = wh[(dj // 4) % 2], wa[(dj // 4) % 2]
                    S.dma("pool", w_h.t[:], Whv[:, :, dj * 128:(dj + 4) * 128], writes=[w_h.r])
                    S.dma("pool", w_a.t[:], Wav[:, :, dj * 128:(dj + 4) * 128], writes=[w_a.r])
                S.dma("sp", gh_[i].t[:], env.uT[s][GOFF + dj * 128:GOFF + (dj + 1) * 128, cols], writes=[gh_[i].r])
                S.dma("sp", ga_[i].t[:], env.uT[s][GOFF + D + dj * 128:GOFF + D + (dj + 1) * 128, cols], writes=[ga_[i].r])
                p1, p2 = ps[(2 * dj) % 4], ps[(2 * dj + 1) % 4]
                jj = dj % 4
                for hf in range(2):
                    S.op("pe", f_mms([(p1[hf].t[:], w_h.t[:, c, jj * 128:(jj + 1) * 128], zf.t[:, c, hf * 512:(hf + 1) * 512],
                                       c == 0, c == 7) for c in range(8)]), reads=[w_h.r, zf.r], writes=[p1[hf].r])
                    S.op("pe", f_mms([(p2[hf].t[:], w_a.t[:, c, jj * 128:(jj + 1) * 128], ya.t[:, c, hf * 512:(hf + 1) * 512],
                                       c == 0, c == 3) for c in range(4)]), reads=[w_a.r, ya.r], writes=[p2[hf].r])
                for hf in range(2):
                    sl = slice(hf * 512, (hf + 1) * 512)
                    S.op("dve", f_tt(t1[i].t[:, sl], p1[hf].t[:], gh_[i].t[:, sl], ALU.mult), reads=[p1[hf].r, gh_[i].r],
                         pwrites=[t1[i].r])
                    S.op("dve", f_tt(t2[i].t[:, sl], p2[hf].t[:], ga_[i].t[:, sl], ALU.mult), reads=[p2[hf].r, ga_[i].r],
                         pwrites=[t2[i].r])
                S.op("pool", f_tt(mg.t[:, dj, :], t1[i].t[:], t2[i].t[:], ALU.add), reads=[t1[i].r, t2[i].r], pwrites=[mg.r])
            for dp in range(16):
                if dp % 4 == 0:
                    w_o = wo[(dp // 4) % 2]
                    S.dma("pool", w_o.t[:], Wov[:, :, dp * 128:(dp + 4) * 128], writes=[w_o.r])
                pp = ps[dp % 4]
                jj = dp % 4
                for hf in range(2):
                    S.op("pe", f_mms([(pp[hf].t[:], w_o.t[:, c, jj * 128:(jj + 1) * 128], mg.t[:, c, hf * 512:(hf + 1) * 512],
                                       c == 0, c == 15) for c in range(16)]), reads=[w_o.r, mg.r], writes=[pp[hf].r])
                residual_tile(S, nc, env, s, pp, dp, tt * TT, TT, m.t[:, l, 32 + dp, s:s + 1], xr[nn % 2], xn[nn % 2])
                nn += 1
        S.barrier()


def phase_ffn2(S, nc, env, s, l):
    TT = 1024
    m = env.modT
    NF = DFF // 128
    WC = 256
    with contextlib.ExitStack() as k:
        hid = Tile(k, nc, [128, NF, TT], BF16)
        gt = [Tile(k, nc, [128, TT + 2], BF16) for _ in range(4)]
        at = [Tile(k, nc, [128, TT], BF16) for _ in range(4)]
        o32 = [Tile(k, nc, [128, TT], F32) for _ in range(3)]
        s32 = [Tile(k, nc, [128, TT], F32) for _ in range(2)]
        wd = [Tile(k, nc, [128, NF, WC], BF16) for _ in range(2)]
        xr = [Tile(k, nc, [128, TT], F32) for _ in range(2)]
        xn = [Tile(k, nc, [128, TT], F32) for _ in range(2)]
        ps = [[Tile(k, nc, [128, 512], F32, psum=True) for _ in range(2)] for _ in range(3)]
        Wdv = env.i_ffn_down[l].rearrange("(kc p) n -> p kc n", p=128)
        w = env.fcw.t
        nn = 0
        ntile = L // TT
        nwt = D // WC

        def load_w(i_):
            wt_ = wd[i_ % 2]
            c_ = (i_ % nwt) * WC
            S.dma("pool", wt_.t[:], Wdv[:, :, c_:c_ + WC], writes=[wt_.r])
        load_w(0)
        load_w(1)
        for tt in range(ntile):
            t0 = tt * TT
            if tt == 0:
                for g_ in gt:
                    S.op("pool", f_memset(g_.t[:, 0:1], 0.0), writes=[g_.r])
            if tt == ntile - 1:
                for g_ in gt:
                    S.op("pool", f_memset(g_.t[:, TT + 1:TT + 2], 0.0), writes=[g_.r])
            def st_load(fk):
                g_, a_ = gt[fk % 4], at[fk % 4]
                lo = max(t0 - 1, 0)
                hi = min(t0 + TT + 1, L)
                S.dma("sp", g_.t[:, lo - (t0 - 1):hi - (t0 - 1)], env.gT[s][fk * 128:(fk + 1) * 128, lo:hi], writes=[g_.r])
                S.dma("sp", a_.t[:], env.aT[s][fk * 128:(fk + 1) * 128, t0:t0 + TT], writes=[a_.r])

            def st_a(fk):
                g_, o_ = gt[fk % 4], o32[fk % 3]
                S.op("act", f_act(o_.t[:], g_.t[:, 1:TT + 1], AF.Identity, bias=env.fcb.t[:, l, fk:fk + 1], scale=w[:, l, fk, 1:2]),
                     reads=[g_.r, env.fcw.r, env.fcb.r], writes=[o_.r])

            def st_b(fk):
                g_, o_ = gt[fk % 4], o32[fk % 3]
                S.op("dve", f_stt(o_.t[:], g_.t[:, 0:TT], w[:, l, fk, 0:1], o_.t[:], ALU.mult, ALU.add), reads=[g_.r, o_.r], writes=[o_.r])
                S.op("dve", f_stt(o_.t[:], g_.t[:, 2:TT + 2], w[:, l, fk, 2:3], o_.t[:], ALU.mult, ALU.add), reads=[g_.r, o_.r], writes=[o_.r])

            def st_c(fk):
                a_, o_, s_ = at[fk % 4], o32[fk % 3], s32[fk % 2]
                S.op("act", f_act(s_.t[:], o_.t[:], AF.Silu), reads=[o_.r], writes=[s_.r])
                S.op("pool", f_tt(hid.t[:, fk, :], s_.t[:], a_.t[:], ALU.mult), reads=[s_.r, a_.r], pwrites=[hid.r])
            st_load(0)
            st_load(1)
            st_a(0)
            for fk in range(NF):
                if fk + 2 < NF:
                    st_load(fk + 2)
                st_b(fk)
                if fk + 1 < NF:
                    st_a(fk + 1)
                st_c(fk)
            per = WC // 128
            for dp in range(16):
                iw = tt * nwt + dp // per
                wt = wd[iw % 2]
                jj = dp % per
                pp = ps[dp % 3]
                for hf in range(2):
                    S.op("pe", f_mms([(pp[hf].t[:], wt.t[:, c, jj * 128:(jj + 1) * 128], hid.t[:, c, hf * 512:(hf + 1) * 512],
                                       c == 0, c == NF - 1) for c in range(NF)]), reads=[wt.r, hid.r], writes=[pp[hf].r])
                if jj == per - 1 and iw + 2 < ntile * nwt:
                    load_w(iw + 2)
                residual_tile(S, nc, env, s, pp, dp, t0, TT, m.t[:, l, 80 + dp, s:s + 1], xr[nn % 2], xn[nn % 2])
                nn += 1
        S.barrier()


def phase_final(S, nc, env, s):
    SUB = 256
    with contextlib.ExitStack() as k:
        x = Tile(k, nc, [128, 16, SUB], F32)
        sq = Tile(k, nc, [128, 16, SUB], BF16)
        rstd = Tile(k, nc, [128, SUB], F32)
        tmp = [Tile(k, nc, [128, SUB], F32) for _ in range(2)]
        pss = Tile(k, nc, [128, 512], F32, psum=True)
        yT = Tile(k, nc, [128, 16, SUB], F32)
        yo = [Tile(k, nc, [128, D], F32) for _ in range(2)]
        ps = [Tile(k, nc, [128, 512], F32, psum=True) for _ in range(4)]
        bufs = (x, sq, pss, rstd, tmp)
        n = 0
        for sb in range(L // SUB):
            t0 = sb * SUB
            norm_subtile(S, nc, env, k, bufs, env.xT[s], t0, SUB, env.fing.t[:], None, None, yT.r,
                         out_f32=(lambda dc: yT.t[:, dc, :]))
            for q in range(SUB // 128):
                o = yo[(sb * 2 + q) % 2]
                for g4 in range(4):
                    p = ps[n % 4]
                    n += 1
                    S.op("pe", f_trs([(p.t[:, j * 128:(j + 1) * 128], yT.t[:, g4 * 4 + j, q * 128:(q + 1) * 128]) for j in range(4)],
                                     env.ident_f.t[:]), reads=[yT.r, env.ident_f.r], writes=[p.r])
                    eng = "act" if g4 % 2 else "dve"
                    S.op(eng, f_copy(eng, o.t[:, g4 * 512:(g4 + 1) * 512], p.t[:]), reads=[p.r], pwrites=[o.r])
                S.dma("sp", env.yout[s, t0 + q * 128:t0 + (q + 1) * 128, :], o.t[:], reads=[o.r])
        S.barrier()


def build(nseq=2, depth=DEPTH, dbg=False, stop_after=None, phases=None, ext_in=()):
    nc = bass.Bass("TRN2", target_bir_lowering=False)
    env = Env()
    env.nseq, env.depth = nseq, depth
    env.dbg = dbg

    def din(name, shape, dt=F32):
        return nc.dram_tensor(name, list(shape), dt, kind="ExternalInput").ap()

    def dscr(name, shape, dt):
        kind = "ExternalInput" if name in ext_in else ("ExternalOutput" if dbg else "Internal")
        h = nc.dram_tensor(name, list(shape), dt, kind=kind)
        return h

    def want(ph):
        return phases is None or ph in phases

    env.xin = din("xin", [nseq, L, D])
    env.i_cT = din("cT", [128, 16, nseq])
    env.i_ada_w = din("ada_w", [depth, D, 6 * D])
    env.i_adab = din("adab", [128, depth, 96])
    env.i_n1g = din("n1g", [128, depth, 16])
    env.i_n2g = din("n2g", [128, depth, 16])
    env.i_fing = din("fing", [128, 16])
    env.i_w_in = din("w_in", [depth, D, INC])
    env.i_bgate = din("bgate", [128, depth, 32])
    env.i_hcw = din("hcw", [128, depth, 24, 3])
    env.i_hcb = din("hcb", [128, depth, 24])
    env.i_fw1 = din("fw1", [depth, 33, 64])
    env.i_fw2 = din("fw2", [depth, 64, 64])
    env.i_fw3 = din("fw3", [depth, 64, 4096])
    env.i_fpar = din("fpar", [depth, 64, 4])
    env.i_hbT = din("hbT", [128, depth, 2, 8])
    env.i_rel_bias = din("rel_bias", [32, 24])
    env.i_w_br_hy = din("w_br_hy", [depth, HW, D])
    env.i_w_br_attn = din("w_br_attn", [depth, 512, D])
    env.i_w_out = din("w_out", [depth, D, D])
    env.i_ffn_up = din("ffn_up", [depth, D, 2 * DFF])
    env.i_fcw = din("fcw", [128, depth, 44, 3])
    env.i_fcb = din("fcb", [128, depth, 44])
    env.i_ffn_down = din("ffn_down", [depth, DFF, D])
    env.i_Fh = din("Fh", [64, 128, 32 * 128], BF16)
    env.i_Gh = din("Gh", [8, 16, 128, 4 * 512], BF16)
    env.i_featT = din("featT", [33, L])
    env.i_decay = din("decay", [L, HW])
    env.i_selw = din("selw", [32, 3, WLEN])
    env.i_ident_f = din("ident_f", [128, 128])
    env.i_ident_b = din("ident_b", [128, 128], BF16)
    env.i_alt = din("alt", [128, 2], BF16)
    env.yout = nc.dram_tensor("yout", [nseq, L, D], F32, kind="ExternalOutput").ap()
    env.xT = dscr("xT", [nseq, D, L], F32).ap()
    env.uT = dscr("uT", [nseq, INC, L], BF16).ap()
    env.ucT = dscr("ucT", [nseq, 3 * HW, L], BF16).ap()
    env.zmid = dscr("zmid", [nseq, HW, L], BF16).ap()
    env.zfT = dscr("zfT", [nseq, HW, L], BF16).ap()
    env.yatT = dscr("yatT", [nseq, 512, L], BF16).ap()
    env.aT = dscr("aT", [nseq, DFF, L], BF16).ap()
    env.gT = dscr("gT", [nseq, DFF, L], BF16).ap()
    env.Kd = dscr("Kd", [2, L, 2 * HW], F32).ap()
    env.Wd_h = dscr("Wd", [24, 128, WLEN], F32)
    if dbg:
        env.dbg_h1 = dscr("dbg_h1", [64, L], F32).ap()
        env.dbg_h2 = dscr("dbg_h2", [64, L], F32).ap()
        env.dbg_e = dscr("dbg_e", [128, 32, 512], BF16).ap()
    env.Wd = env.Wd_h.ap()

    with contextlib.ExitStack() as stk:
        env.pstk = stk
        S = Sched(nc, stk)

        def plan():
            phase_consts(S, nc, env)
            if want("xT"):
                for s in range(nseq):
                    phase_xT(S, nc, env, s)
            if want("mod"):
                phase_mod(S, nc, env)
            if want("eb1"):
                phase_eb1(S, nc, env)
            for l in range(depth):
                if want("filter"):
                    phase_filter(S, nc, env, l)
                for s in range(nseq):
                    if want("proj_in"):
                        phase_proj(S, nc, env, s, l, "in")
                    if stop_after == "proj":
                        return
                    if want("dwconv"):
                        phase_dwconv(S, nc, env, s, l)
                    if want("hyena"):
                        for o in range(2):
                            for hh in range(2):
                                phase_hyena(S, nc, env, s, l, hh, o)
                    if stop_after == "hyena":
                        return
                    if want("attn"):
                        phase_attn(S, nc, env, s, l)
                    if stop_after == "attn":
                        return
                    if want("merge"):
                        phase_merge(S, nc, env, s, l)
                    if stop_after == "merge":
                        return
                    if want("proj_up"):
                        phase_proj(S, nc, env, s, l, "up")
                    if want("ffn2"):
                        phase_ffn2(S, nc, env, s, l)
            if want("final"):
                for s in range(nseq):
                    phase_final(S, nc, env, s)
        plan()
        S.barrier()
        block = stk.enter_context(nc.Block())
        S.emit(block)
    return nc


def t5_bucket(rel):
    nb = 16
    ret = (rel > 0).astype(np.int32) * nb
    n = np.abs(rel)
    max_exact = nb // 2
    large = max_exact + (np.log(np.maximum(n, 1) / max_exact) / np.log(1024 / max_exact) * (nb - max_exact)).astype(np.int32)
    large = np.minimum(large, nb - 1)
    return ret + np.where(n < max_exact, n, large)


_CONST = {}


def host_consts():
    if _CONST:
        return _CONST
    bf = ml_dtypes.bfloat16
    t = np.arange(L, dtype=np.float64)[:, None]
    f = np.arange(L, dtype=np.float64)[None, :]
    ang = 2.0 * np.pi * ((t * f) % NFFT) / NFFT
    C = np.cos(ang)
    Sn = np.sin(ang)
    Sn[:, 0] = (-1.0) ** np.arange(L)
    Fm = np.concatenate([C, Sn], axis=1)
    Fh = Fm.reshape(32, 128, 64, 128).transpose(2, 1, 0, 3).reshape(64, 128, 32 * 128)
    Gc = C.T.copy()
    Gs = Sn.T.copy()
    Gc[0, :] *= 0.5
    Gs[0, :] *= 0.5
    Gm = np.concatenate([Gc, Gs], axis=0) * (2.0 / NFFT)
    Gh = Gm.reshape(16, 4, 128, 8, 512).transpose(3, 0, 2, 1, 4).reshape(8, 16, 128, 4 * 512)
    _CONST["Fh"] = np.ascontiguousarray(Fh).astype(bf)
    _CONST["Gh"] = np.ascontiguousarray(Gh).astype(bf)
    tt = np.linspace(0.0, 1.0, L, dtype=np.float32)[:, None]
    pos = np.arange(L, dtype=np.float32)[:, None]
    bands = np.linspace(1e-4, 15, 16, dtype=np.float32)[None, :]
    angf = (np.float32(2.0 * math.pi / L) * pos * bands).astype(np.float32)
    feat = np.concatenate([tt, np.cos(angf), -np.sin(angf)], axis=-1).astype(np.float32)
    _CONST["featT"] = np.ascontiguousarray(feat.T)
    max_decay = math.log(1e-2) / 0.3
    min_decay = math.log(1e-2) / 1.5
    deltas = np.abs(np.linspace(min_decay, max_decay, HW, dtype=np.float32))
    _CONST["decay"] = np.exp(-tt * deltas).astype(np.float32)
    selw = np.zeros((32, 3, WLEN), np.float32)
    for g, (_, dil) in enumerate(GROUPS):
        for i in range(WLEN):
            j = 191 - i
            if abs(j) <= 64:
                selw[int(t5_bucket(np.array([j * dil]))[0]), g, i] = 1.0
    _CONST["selw"] = selw
    _CONST["ident_f"] = np.eye(128, dtype=np.float32)
    _CONST["ident_b"] = np.eye(128, dtype=np.float32).astype(bf)
    alt = np.stack([(-1.0) ** np.arange(128), np.ones(128)], axis=1).astype(bf)
    _CONST["alt"] = alt
    return _CONST


def pcol(v, n):
    return np.ascontiguousarray(np.asarray(v, np.float32).reshape(n, 128).T)


def shared_inputs(p, depth):
    c = host_consts()
    m = dict(c)
    f32 = np.float32
    m["ada_w"] = np.ascontiguousarray(p["ada_w"][:depth], f32)
    m["adab"] = np.ascontiguousarray(np.stack([pcol(p["ada_b"][l], 96) for l in range(depth)], axis=1))
    m["n1g"] = np.ascontiguousarray(np.stack([pcol(p["norm1_g"][l], 16) for l in range(depth)], axis=1))
    m["n2g"] = np.ascontiguousarray(np.stack([pcol(p["norm2_g"][l], 16) for l in range(depth)], axis=1))
    m["fing"] = pcol(p["final_g"], 16)
    m["w_in"] = np.ascontiguousarray(p["w_in"][:depth], f32)
    m["bgate"] = np.ascontiguousarray(np.stack([pcol(p["b_gate"][l], 32) for l in range(depth)], axis=1))
    m["hcw"] = np.ascontiguousarray(np.stack(
        [np.asarray(p["hy_conv_w"][l], f32).T.reshape(24, 128, 3).transpose(1, 0, 2) for l in range(depth)], axis=1))
    m["hcb"] = np.ascontiguousarray(np.stack([pcol(p["hy_conv_b"][l], 24) for l in range(depth)], axis=1))
    m["fw1"] = np.ascontiguousarray(p["filt_w1"][:depth], f32)
    m["fw2"] = np.ascontiguousarray(p["filt_w2"][:depth], f32)
    m["fw3"] = np.ascontiguousarray(p["filt_w3"][:depth], f32)
    m["fpar"] = np.ascontiguousarray(np.stack(
        [np.stack([p["filt_b1"][l], p["filt_freq1"][l], p["filt_b2"][l], p["filt_freq2"][l]], axis=1) for l in range(depth)],
        axis=0).astype(f32))
    m["hbT"] = np.ascontiguousarray(np.stack(
        [np.asarray(p["hy_bias"][l], f32).reshape(2, 8, 128).transpose(2, 0, 1) for l in range(depth)], axis=1))
    m["rel_bias"] = np.ascontiguousarray(p["rel_bias"], f32)
    m["w_br_hy"] = np.ascontiguousarray(p["w_br_hy"][:depth], f32)
    m["w_br_attn"] = np.ascontiguousarray(p["w_br_attn"][:depth], f32)
    m["w_out"] = np.ascontiguousarray(p["w_out"][:depth], f32)
    m["ffn_up"] = np.ascontiguousarray(p["ffn_up"][:depth], f32)
    m["fcw"] = np.ascontiguousarray(np.stack(
        [np.asarray(p["ffn_conv_w"][l], f32).T.reshape(44, 128, 3).transpose(1, 0, 2) for l in range(depth)], axis=1))
    m["fcb"] = np.ascontiguousarray(np.stack([pcol(p["ffn_conv_b"][l], 44) for l in range(depth)], axis=1))
    m["ffn_down"] = np.ascontiguousarray(p["ffn_down"][:depth], f32)
    return m


def core_inputs(shared, xs, cs):
    m = dict(shared)
    m["xin"] = np.ascontiguousarray(np.stack(xs, axis=0), np.float32)
    cT = np.stack([np.asarray(c, np.float32).reshape(16, 128).T for c in cs], axis=2)
    m["cT"] = np.ascontiguousarray(cT)
    return m


_NC_CACHE = {}


def kernel(x_prompt, x_sample, c_prompt, c_sample, **p):
    x_prompt = np.asarray(x_prompt)
    x_sample = np.asarray(x_sample)
    c_prompt = np.asarray(c_prompt)
    c_sample = np.asarray(c_sample)
    p = {k_: np.asarray(v) for k_, v in p.items()}
    if "nc" not in _NC_CACHE:
        _NC_CACHE["nc"] = build(2, DEPTH)
    nc = _NC_CACHE["nc"]
    shared = shared_inputs(p, DEPTH)
    in_maps = []
    for i in range(8):
        in_maps.append(core_inputs(shared, [x_sample[i], x_prompt[i % 4]], [c_sample[i], c_prompt[i % 4]]))
    res = run_bass_kernel_spmd(nc, in_maps, core_ids=list(range(8)))
    y_sample = np.stack([res.results[i]["yout"][0] for i in range(8)], axis=0).astype(np.float32)
    y_prompt = np.stack([res.results[i]["yout"][1] for i in range(4)], axis=0).astype(np.float32)
    return (y_prompt, y_sample)
```
